# Optimizing a Trainium2 kernel written in Bass

```python
import math
import jax, jax.numpy as jnp
from jax import lax
import numpy as np

D_MODEL = 1024
BATCH = 16
SEQ = 256
DEPTH = 2
DEC_BATCH = 4
DEC_SEQ = 4096
PAST_LEN = 256

GRID_W = 64
N_AB_LAYERS = (DEPTH + 1) // 2
N_C_LAYERS = DEPTH // 2
CONV_CH = D_MODEL // 2
CONV_WIDTH = 31
M_HEADS = 4
M_DK = (D_MODEL // 2) // M_HEADS
M_DV = (D_MODEL // 2) // M_HEADS
M_WIDTH = M_HEADS * M_DV
CHUNK = 64
FORGET_BIAS = 3.0
AB_IN = 2 * CONV_CH + 4 * M_WIDTH + 4 * M_HEADS
HEAD_DIM = 128
N_Q = D_MODEL // HEAD_DIM
N_KV = 2
GROUP = N_Q // N_KV
C_IN = (N_Q + 2 * N_KV) * HEAD_DIM
Q_BLOCK = 128
ROPE_THETA = 10000.0
D_FF = -(-8 * D_MODEL // (3 * 256)) * 256
EPS = 1e-6

kernel_name = 'hybrid_diffusion_conv_mlstm_gqa_step'


def rms_norm(x, g):
    x32 = x.astype(jnp.float32)
    y = x32 * lax.rsqrt(jnp.mean(x32 * x32, axis=-1, keepdims=True) + EPS)
    return (y * g.astype(jnp.float32)).astype(x.dtype)


def layer_norm(x, g, b):
    x32 = x.astype(jnp.float32)
    mu = jnp.mean(x32, axis=-1, keepdims=True)
    var = jnp.mean(jnp.square(x32 - mu), axis=-1, keepdims=True)
    y = (x32 - mu) * lax.rsqrt(var + EPS)
    return (y * g.astype(jnp.float32) + b.astype(jnp.float32)).astype(x.dtype)


def adaln(cond, w, b):
    mod = jax.nn.silu(cond) @ w + b
    return jnp.split(mod[:, None, :], 6, axis=-1)


def modulate(x, g, shift, scale):
    return rms_norm(x, g) * (1 + scale) + shift


def swiglu(h, w_in, w_out):
    gt, up = jnp.split(h @ w_in, 2, axis=-1)
    return (jax.nn.silu(gt) * up) @ w_out


def depthwise_conv(x, w, b):
    y = lax.conv_general_dilated(
        x, w[:, None, :], window_strides=(1,),
        padding=[(CONV_WIDTH // 2, CONV_WIDTH // 2)],
        dimension_numbers=('NWC', 'WIO', 'NWC'),
        feature_group_count=x.shape[-1])
    return y + b


def mlstm_scan(q, k, v, log_i, log_f, state):
    B, S, H, _ = q.shape
    DV = v.shape[-1]
    nc = S // CHUNK

    def chunks(a):
        a = a.reshape((B, nc, CHUNK, H) + a.shape[3:])
        return jnp.moveaxis(a, 3, 2).swapaxes(0, 1)

    mask = jnp.tril(jnp.ones((CHUNK, CHUNK), dtype=bool))

    def step(carry, xs):
        C, n, m = carry
        qc, kc, vc, ic, fc = xs
        b = jnp.cumsum(fc, axis=-1)
        logd = jnp.where(mask, b[..., :, None] - b[..., None, :] + ic[..., None, :], -jnp.inf)
        inter = b + m[..., None]
        m_t = jnp.maximum(inter, jnp.max(logd, axis=-1))
        dmat = jnp.exp(logd - m_t[..., None])
        w_inter = jnp.exp(inter - m_t)
        s = jnp.einsum('bhtd,bhsd->bhts', qc, kc) * dmat
        num = jnp.einsum('bhts,bhsv->bhtv', s, vc) + w_inter[..., None] * jnp.einsum('bhtd,bhdv->bhtv', qc, C)
        den = jnp.sum(s, axis=-1) + w_inter * jnp.einsum('bhtd,bhd->bht', qc, n)
        h = num / jnp.maximum(jnp.abs(den), jnp.exp(-m_t))[..., None]
        b_end = b[..., -1]
        log_w = b_end[..., None] - b + ic
        m_new = jnp.maximum(b_end + m, jnp.max(log_w, axis=-1))
        w = jnp.exp(log_w - m_new[..., None])
        a = jnp.exp(b_end + m - m_new)
        C_new = a[..., None, None] * C + jnp.einsum('bhsd,bhsv->bhdv', kc * w[..., None], vc)
        n_new = a[..., None] * n + jnp.einsum('bhs,bhsd->bhd', w, kc)
        return (C_new, n_new, m_new), h

    state, h = lax.scan(step, state, (chunks(q), chunks(k), chunks(v), chunks(log_i), chunks(log_f)))
    h = jnp.moveaxis(h.swapaxes(0, 1), 2, 3).reshape(B, S, H, DV)
    return h, state


def mlstm_bidir(q, k, v, gates, state_f, state_b):
    i_f, f_f, i_b, f_b = jnp.split(gates, 4, axis=-1)
    h_f, st_f = mlstm_scan(q, k, v, i_f, jax.nn.log_sigmoid(f_f), state_f)
    flip = lambda a: jnp.flip(a, axis=1)
    h_b, st_b = mlstm_scan(flip(q), flip(k), flip(v), flip(i_b), flip(jax.nn.log_sigmoid(f_b)), state_b)
    return h_f + flip(h_b), st_f, st_b


def ab_mixer(h, w_in, b_gate, conv_w, conv_b, ln_g, ln_b, mnorm_g, w_out, state_f, state_b):
    B, S, _ = h.shape
    p = h @ w_in
    cuts = [CONV_CH, 2 * CONV_CH, 2 * CONV_CH + M_WIDTH, 2 * CONV_CH + 2 * M_WIDTH,
            2 * CONV_CH + 3 * M_WIDTH, 2 * CONV_CH + 4 * M_WIDTH]
    ga, gg, q, k, v, o, gates = jnp.split(p, cuts, axis=-1)
    u = ga * jax.nn.sigmoid(gg)
    u = jax.nn.silu(layer_norm(depthwise_conv(u, conv_w, conv_b), ln_g, ln_b))
    f32 = jnp.float32
    q32 = q.reshape(B, S, M_HEADS, M_DK).astype(f32)
    k32 = k.reshape(B, S, M_HEADS, M_DK).astype(f32) * (M_DK ** -0.5)
    v32 = v.reshape(B, S, M_HEADS, M_DV).astype(f32)
    g32 = (gates + b_gate).astype(f32)
    hm, st_f, st_b = mlstm_bidir(q32, k32, v32, g32, state_f, state_b)
    hm = hm * lax.rsqrt(jnp.mean(hm * hm, axis=-1, keepdims=True) + EPS)
    hm = (hm * mnorm_g.astype(f32).reshape(M_HEADS, M_DV)).reshape(B, S, M_WIDTH).astype(h.dtype)
    hm = jax.nn.sigmoid(o) * hm
    out = jnp.concatenate([u, hm], axis=-1) @ w_out
    return out, st_f, st_b


def rope_2d(x):
    S = x.shape[1]
    rows = S // GRID_W
    t_row = jnp.repeat(jnp.arange(rows), GRID_W)
    t_col = jnp.tile(jnp.arange(GRID_W), rows)
    n_freq = HEAD_DIM // 4
    inv = 1.0 / (ROPE_THETA ** (jnp.arange(n_freq, dtype=jnp.float32) / n_freq))

    def rot(xa, pos):
        ang = pos.astype(jnp.float32)[:, None] * inv
        cos = jnp.cos(ang)[None, :, None, :]
        sin = jnp.sin(ang)[None, :, None, :]
        x1, x2 = jnp.split(xa, 2, axis=-1)
        return jnp.concatenate([x1 * cos - x2 * sin, x1 * sin + x2 * cos], axis=-1)

    xr, xc = jnp.split(x.astype(jnp.float32), 2, axis=-1)
    return jnp.concatenate([rot(xr, t_row), rot(xc, t_col)], axis=-1).astype(x.dtype)


def blocked_attention(q, k, v):
    B, Sq = q.shape[:2]
    nb = Sq // Q_BLOCK
    qb = q.reshape(B, nb, Q_BLOCK, N_KV, GROUP, HEAD_DIM).swapaxes(0, 1)
    scale = HEAD_DIM ** -0.5

    def one(qblk):
        s = jnp.einsum('bqhgd,bshd->bhgqs', qblk, k).astype(jnp.float32) * scale
        p = jax.nn.softmax(s, axis=-1).astype(v.dtype)
        return jnp.einsum('bhgqs,bshd->bqhgd', p, v)

    o = lax.map(one, qb)
    return o.swapaxes(0, 1).reshape(B, Sq, N_Q * HEAD_DIM)


def qkv_proj(h, w_in, qg, kg):
    B, S, _ = h.shape
    q, k, v = jnp.split(h @ w_in, [N_Q * HEAD_DIM, (N_Q + N_KV) * HEAD_DIM], axis=-1)
    q = rms_norm(q.reshape(B, S, N_Q, HEAD_DIM), qg)
    k = rms_norm(k.reshape(B, S, N_KV, HEAD_DIM), kg)
    return q, k, v.reshape(B, S, N_KV, HEAD_DIM)


def attn_context(h, w_in, qg, kg, w_out):
    B, S, _ = h.shape
    q, k, v = qkv_proj(h, w_in, qg, kg)
    o = blocked_attention(q.reshape(B, S, N_KV, GROUP, HEAD_DIM), k, v)
    return o @ w_out, k, v


def attn_latent(h, w_in, qg, kg, w_out, k_ctx, v_ctx):
    B, S, _ = h.shape
    q, k, v = qkv_proj(h, w_in, qg, kg)
    q, k = rope_2d(q), rope_2d(k)
    k_all = jnp.concatenate([k, k_ctx.astype(k.dtype)], axis=1)
    v_all = jnp.concatenate([v, v_ctx.astype(v.dtype)], axis=1)
    o = blocked_attention(q.reshape(B, S, N_KV, GROUP, HEAD_DIM), k_all, v_all)
    return o @ w_out


def setup_inputs(seed: int = 0) -> dict:
    key = jax.random.key(seed)
    ks = iter(jax.random.split(key, 40))

    def nrm(shape, scale=1.0):
        return scale * jax.random.normal(next(ks), shape, jnp.float32)

    D = D_MODEL
    gate_offset = jnp.tile(jnp.concatenate([jnp.zeros((M_HEADS,), jnp.float32),
                                            jnp.full((M_HEADS,), FORGET_BIAS, jnp.float32)]), 2)
    return {
        'x_prompt': nrm((BATCH, SEQ, D)),
        'x_sample': nrm((DEC_BATCH, DEC_SEQ, D)),
        'c': nrm((DEC_BATCH, D)),
        'state_mlstm_C': nrm((DEC_BATCH, N_AB_LAYERS, 2, M_HEADS, M_DK, M_DV), 0.5),
        'state_mlstm_n': nrm((DEC_BATCH, N_AB_LAYERS, 2, M_HEADS, M_DK), 0.5),
        'state_mlstm_m': nrm((DEC_BATCH, N_AB_LAYERS, 2, M_HEADS)),
        'cache_k': nrm((DEC_BATCH, N_C_LAYERS, PAST_LEN, N_KV, HEAD_DIM)),
        'cache_v': nrm((DEC_BATCH, N_C_LAYERS, PAST_LEN, N_KV, HEAD_DIM)),
        'c_ctx': nrm((D,)),
        'w_mod': nrm((DEPTH, D, 6 * D), D ** -0.5),
        'b_mod': nrm((DEPTH, 6 * D), 0.02),
        'norm1_g': 1.0 + nrm((DEPTH, D), 0.02),
        'norm2_g': 1.0 + nrm((DEPTH, D), 0.02),
        'w_in_ab': nrm((N_AB_LAYERS, D, AB_IN), D ** -0.5),
        'b_gate_ab': gate_offset + nrm((N_AB_LAYERS, 4 * M_HEADS), 0.1),
        'conv_w': nrm((N_AB_LAYERS, CONV_WIDTH, CONV_CH), CONV_WIDTH ** -0.5),
        'conv_b': nrm((N_AB_LAYERS, CONV_CH), 0.02),
        'conv_ln_g': 1.0 + nrm((N_AB_LAYERS, CONV_CH), 0.02),
        'conv_ln_b': nrm((N_AB_LAYERS, CONV_CH), 0.02),
        'mlstm_norm_g': 1.0 + nrm((N_AB_LAYERS, M_WIDTH), 0.02),
        'w_out_ab': nrm((N_AB_LAYERS, CONV_CH + M_WIDTH, D), (CONV_CH + M_WIDTH) ** -0.5),
        'w_in_c': nrm((N_C_LAYERS, D, C_IN), D ** -0.5),
        'q_norm_g': 1.0 + nrm((N_C_LAYERS, HEAD_DIM), 0.02),
        'k_norm_g': 1.0 + nrm((N_C_LAYERS, HEAD_DIM), 0.02),
        'w_out_c': nrm((N_C_LAYERS, N_Q * HEAD_DIM, D), (N_Q * HEAD_DIM) ** -0.5),
        'w_ffn_in': nrm((DEPTH, D, 2 * D_FF), D ** -0.5),
        'w_ffn_out': nrm((DEPTH, D_FF, D), D_FF ** -0.5),
        'final_norm_g': 1.0 + nrm((D,), 0.02),
    }


def reference(x_prompt, x_sample, c, state_mlstm_C, state_mlstm_n, state_mlstm_m, cache_k, cache_v,
              c_ctx, w_mod, b_mod, norm1_g, norm2_g, w_in_ab, b_gate_ab, conv_w, conv_b, conv_ln_g,
              conv_ln_b, mlstm_norm_g, w_out_ab, w_in_c, q_norm_g, k_norm_g, w_out_c, w_ffn_in,
              w_ffn_out, final_norm_g):
    f32 = jnp.float32
    ctx, lat = x_prompt, x_sample
    Bc = ctx.shape[0]
    zero_state = (jnp.zeros((Bc, M_HEADS, M_DK, M_DV), f32),
                  jnp.zeros((Bc, M_HEADS, M_DK), f32),
                  jnp.zeros((Bc, M_HEADS), f32))
    new_C, new_n, new_m, new_k, new_v = [], [], [], [], []
    for l in range(DEPTH):
        sh1c, sc1c, g1c, sh2c, sc2c, g2c = adaln(c_ctx[None, :], w_mod[l], b_mod[l])
        sh1, sc1, g1, sh2, sc2, g2 = adaln(c, w_mod[l], b_mod[l])
        hc = modulate(ctx, norm1_g[l], sh1c, sc1c)
        hl = modulate(lat, norm1_g[l], sh1, sc1)
        j = l // 2
        if l % 2 == 0:
            ab_w = (w_in_ab[j], b_gate_ab[j], conv_w[j], conv_b[j], conv_ln_g[j], conv_ln_b[j],
                    mlstm_norm_g[j], w_out_ab[j])
            oc, sf, sb = ab_mixer(hc, *ab_w, zero_state, zero_state)
            new_C.append(jnp.stack([sf[0], sb[0]], axis=1))
            new_n.append(jnp.stack([sf[1], sb[1]], axis=1))
            new_m.append(jnp.stack([sf[2], sb[2]], axis=1))
            lat_f = (state_mlstm_C[:, j, 0].astype(f32), state_mlstm_n[:, j, 0].astype(f32),
                     state_mlstm_m[:, j, 0].astype(f32))
            lat_b = (state_mlstm_C[:, j, 1].astype(f32), state_mlstm_n[:, j, 1].astype(f32),
                     state_mlstm_m[:, j, 1].astype(f32))
            ol, _, _ = ab_mixer(hl, *ab_w, lat_f, lat_b)
        else:
            oc, kc, vc = attn_context(hc, w_in_c[j], q_norm_g[j], k_norm_g[j], w_out_c[j])
            new_k.append(kc)
            new_v.append(vc)
            ol = attn_latent(hl, w_in_c[j], q_norm_g[j], k_norm_g[j], w_out_c[j],
                             cache_k[:, j], cache_v[:, j])
        ctx = ctx + g1c * oc
        lat = lat + g1 * ol
        ctx = ctx + g2c * swiglu(modulate(ctx, norm2_g[l], sh2c, sc2c), w_ffn_in[l], w_ffn_out[l])
        lat = lat + g2 * swiglu(modulate(lat, norm2_g[l], sh2, sc2), w_ffn_in[l], w_ffn_out[l])
    y_prompt = rms_norm(ctx, final_norm_g)
    y_sample = rms_norm(lat, final_norm_g)
    dt = x_prompt.dtype
    return (y_prompt, y_sample,
            jnp.stack(new_C, axis=1).astype(dt), jnp.stack(new_n, axis=1).astype(dt),
            jnp.stack(new_m, axis=1).astype(dt), jnp.stack(new_k, axis=1), jnp.stack(new_v, axis=1))
```

```python
import numpy as np
from concourse.bass_utils import run_bass_kernel_spmd

class R:
    __slots__ = ("buf", "ap", "key")

    def __init__(self, buf, ap, key=None):
        self.buf = buf
        self.ap = ap
        self.key = key


class Buf:
    def __init__(self, t, name, is_ap=False):
        self.t = t
        self.name = name
        self.w = {}
        self.r = {}
        self.is_ap = is_ap

    def __call__(self, idx=None, key=None):
        if idx is None:
            ap = self.t if self.is_ap else self.t[:]
        else:
            ap = self.t[idx]
        return R(self, ap, key)

    def v(self, ap, key=None):
        return R(self, ap, key)


class _Rec:
    def __init__(self):
        self.call = None

    def __getattr__(self, name):
        def f(*a, **kw):
            assert self.call is None
            self.call = (name, a, kw)
            return None
        return f


def _eager(fn):
    r = _Rec()
    fn(r)
    name, a, kw = r.call
    return lambda e: getattr(e, name)(*a, **kw)


class Op:
    __slots__ = ("eng", "fn", "deps", "sig", "val", "kind", "sem", "n")

    def __init__(self, eng, fn, kind):
        self.eng = eng
        self.fn = fn
        self.deps = []
        self.sig = False
        self.val = 0
        self.kind = kind
        self.sem = None
        self.n = 0


def _conf(k1, k2):
    return k1 is None or k2 is None or k1 == k2


class Pool:
    def __init__(self, bufs):
        self.bufs = bufs
        self.i = 0

    def next(self):
        b = self.bufs[self.i % len(self.bufs)]
        self.i += 1
        return b


class Sched:
    ENGS = ("pe", "act", "dve", "pool", "sp")

    def __init__(self, nc, es, ndma=None):
        self.nc = nc
        self.es = es
        self.ops = {e: [] for e in self.ENGS}
        self.nops = 0
        self.csem = {}
        for e in ("pe", "act", "dve", "pool"):
            self.csem[e] = es.enter_context(nc.semaphore("c_" + e))
        ndma = ndma or {"sp": 20, "pool": 12, "act": 6}
        self.dsem = {}
        self.dcnt = {}
        self.dhist = {}
        for q, n in ndma.items():
            self.dsem[q] = [es.enter_context(nc.semaphore("d_%s%d" % (q, i))) for i in range(n)]
            self.dcnt[q] = 0
            self.dhist[q] = []
        self.ccsem = es.enter_context(nc.semaphore("ccs"))
        self.cccnt = 0
        self.all_dma = []
        self.scopes = []

    def sb(self, name, shape, dt):
        st = self.scopes[-1] if self.scopes else self.es
        self.uid = getattr(self, "uid", 0) + 1
        name = "%s_u%d" % (name, self.uid)
        return Buf(st.enter_context(self.nc.sbuf_tensor(name, list(shape), dt)), name)

    def scope(self):
        import contextlib
        K = self

        @contextlib.contextmanager
        def cm():
            st = contextlib.ExitStack()
            K.scopes.append(st)
            try:
                yield
            finally:
                K.scopes.pop()
                K.fence()
                st.close()
        return cm()

    def push_scope(self):
        import contextlib
        self.scopes.append(contextlib.ExitStack())

    def pop_scope(self):
        st = self.scopes.pop()
        self.fence()
        st.close()

    def fence(self):
        lastc = []
        for e in ("pe", "act", "dve", "pool"):
            for x in reversed(self.ops[e]):
                if x.kind == "c":
                    lastc.append(x)
                    break
        lastd = []
        for q in self.dhist:
            ns = len(self.dsem[q])
            lastd.extend(self.dhist[q][-ns:])
        lastcc = None
        for x in reversed(self.ops["pool"]):
            if x.kind == "cc":
                lastcc = x
                break
        for e in self.ENGS:
            o = Op(e, None, "nop")
            for x in lastc:
                x.sig = True
                o.deps.append(x)
            o.deps.extend(lastd)
            if lastcc is not None:
                o.deps.append(lastcc)
            self.ops[e].append(o)

    def ps(self, name, shape, dt):
        return Buf(self.es.enter_context(self.nc.psum_tensor(name, list(shape), dt)), name)

    def dram(self, name, shape, dt, kind="Internal"):
        return Buf(self.nc.dram_tensor(name, list(shape), dt, kind=kind).ap(), name, is_ap=True)

    def pool(self, name, shape, dt, n, space="sb"):
        mk = self.sb if space == "sb" else self.ps
        return Pool([mk("%s%d" % (name, i), shape, dt) for i in range(n)])

    def _track(self, op, reads, writes):
        deps = op.deps
        for r in reads:
            if r is None or r.buf is None:
                continue
            b = r.buf
            for k2, w in b.w.items():
                if _conf(r.key, k2):
                    deps.append(w)
        for w_ in writes:
            b = w_.buf
            for k2, w in b.w.items():
                if _conf(w_.key, k2):
                    deps.append(w)
            for k2, rl in b.r.items():
                if _conf(w_.key, k2):
                    deps.extend(rl)
        for r in reads:
            if r is None or r.buf is None:
                continue
            r.buf.r.setdefault(r.key, []).append(op)
        for w_ in writes:
            b = w_.buf
            if w_.key is None:
                b.w = {None: op}
                b.r = {}
            else:
                b.w[w_.key] = op
                b.r[w_.key] = []
        out = []
        seen = set()
        for d in deps:
            if d is op or id(d) in seen:
                continue
            seen.add(id(d))
            if d.eng == "pe" and op.eng == "pe" and d.kind == "c" and op.kind == "c":
                continue
            out.append(d)
            d.sig = True
        op.deps = out

    def op(self, eng, fn, reads=(), writes=()):
        o = Op(eng, _eager(fn), "c")
        o.sem = self.csem[eng]
        o.n = self.nops
        self.nops += 1
        self._track(o, reads, writes)
        self.ops[eng].append(o)
        return o

    def dma(self, q, out, in_, extra_reads=(), **kw):
        o = Op(q, (lambda e, o_=out.ap, i_=in_.ap, kw_=kw: e.dma_start(out=o_, in_=i_, **kw_)), "dma")
        o.n = self.nops
        self.nops += 1
        n = self.dcnt[q]
        ns = len(self.dsem[q])
        o.sem = self.dsem[q][n % ns]
        o.val = 16 * (n // ns + 1)
        self.dcnt[q] += 1
        self._track(o, [in_] + list(extra_reads), [out])
        if n >= ns:
            prev = self.dhist[q][n - ns]
            if prev not in o.deps:
                o.deps.append(prev)
        self.dhist[q].append(o)
        self.ops[q].append(o)
        self.all_dma.append(o)
        return o

    def cc(self, fn, reads, writes):
        o = Op("pool", _eager(fn), "cc")
        o.n = self.nops
        self.nops += 1
        o.sem = self.ccsem
        self.cccnt += 1
        o.val = self.cccnt
        self._track(o, reads, writes)
        self.ops["pool"].append(o)
        return o

    def barrier_final(self):
        o = Op("sp", None, "nop")
        for q in self.dhist:
            ns = len(self.dsem[q])
            for d in self.dhist[q][-ns:]:
                o.deps.append(d)
        for e in ("pe", "act", "dve", "pool"):
            for x in reversed(self.ops[e]):
                if x.kind == "c":
                    x.sig = True
                    o.deps.append(x)
                    break
        if self.cccnt:
            for x in reversed(self.ops["pool"]):
                if x.kind == "cc":
                    o.deps.append(x)
                    break
        self.ops["sp"].append(o)

    def emit(self):
        nc = self.nc
        for e in ("pe", "act", "dve", "pool"):
            c = 0
            for o in self.ops[e]:
                if o.kind == "c" and o.sig:
                    c += 1
                    o.val = c
        self.stats = {e: len(self.ops[e]) for e in self.ENGS}
        nwaits = {e: 0 for e in self.ENGS}

        def run(engname):
            def f(eng):
                seen = {}
                for o in self.ops[engname]:
                    for d in o.deps:
                        k = id(d.sem)
                        if seen.get(k, 0) < d.val:
                            eng.wait_ge(d.sem, d.val)
                            seen[k] = d.val
                            nwaits[engname] += 1
                    if o.kind == "nop":
                        continue
                    ins = o.fn(eng)
                    if o.kind == "dma":
                        ins.then_inc(o.sem, 16)
                    elif o.kind == "cc":
                        ins.then_inc(o.sem, 1)
                    elif o.sig:
                        ins.then_inc(o.sem, 1)
            return f

        with nc.Block() as block:
            block.tensor(run("pe"))
            block.scalar(run("act"))
            block.vector(run("dve"))
            block.gpsimd(run("pool"))
            block.sync(run("sp"))
        self.stats["waits"] = nwaits

import numpy as np
from contextlib import ExitStack
import concourse.bass as bass
import concourse.mybir as mybir

F32 = mybir.dt.float32
BF16 = mybir.dt.bfloat16
AF = mybir.ActivationFunctionType
ALU = mybir.AluOpType
AX = mybir.AxisListType

D = 1024
NLT = 16
NCT = 4
NTT = 20
NT = 21
NTOK = NTT * 128
DFF = 2816
NFB = 22
EPS = 1e-6
NKEY = 4096 + 256
NKC = NKEY // 128

STAGE = 99


def build(nc, stage=99, debug=False):
    es = ExitStack()
    K = Sched(nc, es)
    K.debug = debug
    IN = {}

    def inp(name, shape, dt=F32):
        b = Buf(nc.dram_tensor(name, list(shape), dt, kind="ExternalInput").ap(), name, is_ap=True)
        IN[name] = b
        return b

    def outp(name, shape):
        return Buf(nc.dram_tensor(name, list(shape), F32, kind="ExternalOutput").ap(), name, is_ap=True)

    xs = inp("xs", [NT * 128, D])
    cT = inp("cT", [128, 8, 2])
    st_C = inp("st_C", [128, 4, 129])
    st_m = inp("st_m", [4, 1])
    sel = inp("sel", [128, 2])
    ckT = inp("ckT", [128, 2, 256])
    cv = inp("cv", [128, 2, 2, 128])
    rcos = inp("rcos", [128, NLT, 64])
    rsin = inp("rsin", [128, NLT, 64])
    ident_in = inp("ident", [128, 128])
    maskA_in = inp("maskA", [128, 128])
    maskB_in = inp("maskB", [128, 128])
    wmodA = inp("wmodA", [2, 32, 128, 8, 128])
    bmodA = inp("bmodA", [128, 2, 32])
    wmodG = inp("wmodG", [2, 4, 128, 8, 512])
    bmodG = inp("bmodG", [2, 128, 2048])
    n1g = inp("n1g", [128, 2, 8])
    n2g = inp("n2g", [128, 2, 8])
    w_ab_f = inp("w_ab_f", [18, 128, 8, 128])
    w_ab_t = inp("w_ab_t", [3, 128, 8, 512])
    bg = inp("bg", [128, 2])
    cw = inp("cw", [128, 4, 31])
    cb = inp("cb", [128, 4])
    lng = inp("lng", [128, 4])
    lnb = inp("lnb", [128, 4])
    mng = inp("mng", [128, 512])
    w_oab = inp("w_oab", [2, 128, 8, 512])
    w_c_q = inp("w_c_q", [2, 128, 8, 512])
    w_c_kv = inp("w_c_kv", [128, 8, 512])
    qg = inp("qg", [128, 128])
    kg = inp("kg", [128, 128])
    w_oc = inp("w_oc", [2, 128, 8, 512])
    w_fi = inp("w_fi", [2, NFB, 128, 8, 256])
    w_fo = inp("w_fo", [2, 2, 128, NFB, 512])
    fng = inp("fng", [128, D])

    y = outp("y", [NTOK, D])
    oC = outp("oC", [2, 2, 4, 128, 128])
    on = outp("on", [2, 2, 4, 128])
    om = outp("om", [2, 2, 4])
    ok_ = outp("ok", [512, 256])
    ov_ = outp("ov", [512, 256])

    _kd = K.dram

    def _dram_dbg(name, shape, dt, kind="Internal"):
        if debug and name.startswith("S_"):
            kind = "ExternalOutput"
        return _kd(name, shape, dt, kind=kind)
    K.dram = _dram_dbg
    DBG = {}

    def dump(name, buf):
        if not debug:
            return
        shp = list(buf.t.shape)
        o = Buf(nc.dram_tensor("D_" + name, shp, buf.t.dtype, kind="ExternalOutput").ap(), "D_" + name, is_ap=True)
        K.dma("sp", o(), buf())

    def wscr(src):
        return K.dram(src.name + "_b", list(src.t.shape), BF16)

    WB = {}
    for b_ in (wmodA, wmodG, w_ab_f, w_ab_t, w_oab, w_c_q, w_c_kv, w_oc, w_fi, w_fo):
        WB[b_.name] = wscr(b_)

    def cast_weight(src, pieces):
        dst = WB[src.name]
        shp = list(src.t.shape)
        n0 = shp[0]
        assert n0 % pieces == 0
        st = n0 // pieces
        for i in range(pieces):
            K.dma("pool", dst(np.s_[i * st:(i + 1) * st], key=("w", i)), src(np.s_[i * st:(i + 1) * st]))

    S_qT = K.dram("S_qT", [128, 4, NTOK], BF16)
    S_kT = K.dram("S_kT", [128, 4, NTOK], BF16)
    S_k = K.dram("S_k", [NTOK, 512], BF16)
    S_v = K.dram("S_v", [NTOK, 4, 129], BF16)
    S_o = K.dram("S_o", [NTOK, 512], BF16)
    S_hf = K.dram("S_hf", [NTOK, 512], F32)
    S_act = K.dram("S_act", [128, 8, NTOK], BF16)
    S_x1 = K.dram("S_x1", [NTOK, D], F32)
    S_kv = K.dram("S_kv", [4096, 256], BF16)
    S_kvg = K.dram("S_kvg", [8192, 256], BF16)
    S_st = K.dram("S_st", [128, 520], F32)
    S_stg = K.dram("S_stg", [256, 520], F32)

    def cload(src, shape, dt=F32, name=None):
        b = K.sb(name or ("c_" + src.name), shape, dt)
        K.dma("sp", b(), src())
        return b

    identf = cload(ident_in, [128, 128])
    maskA = cload(maskA_in, [128, 128])
    maskB = cload(maskB_in, [128, 128])
    identb = K.sb("identb", [128, 128], BF16)
    K.op("dve", lambda e: e.tensor_copy(out=identb.t[:], in_=identf.t[:]), [identf()], [identb()])
    onesf = K.sb("onesf", [128, 128], F32)
    K.op("dve", lambda e: e.memset(onesf.t[:], 1.0), [], [onesf()])
    onesb = K.sb("onesb", [128, 128], BF16)
    K.op("dve", lambda e: e.memset(onesb.t[:], 1.0), [], [onesb()])

    cT_s = cload(cT, [128, 8, 2])
    bmodA_s = cload(bmodA, [128, 2, 32])
    n1g_s = cload(n1g, [128, 2, 8])
    n2g_s = cload(n2g, [128, 2, 8])
    bg_s = cload(bg, [128, 2])
    cw_s = cload(cw, [128, 4, 31])
    cb_s = cload(cb, [128, 4])
    lng_s = cload(lng, [128, 4])
    lnb_s = cload(lnb, [128, 4])
    mng_s = cload(mng, [128, 512])
    qg_s = cload(qg, [128, 128])
    kg_s = cload(kg, [128, 128])
    fng_s = cload(fng, [128, D])
    sel_s = cload(sel, [128, 2])
    rcos_s = cload(rcos, [128, NLT, 64])
    rsin_s = cload(rsin, [128, NLT, 64])

    cast_weight(wmodA, 2)
    cast_weight(wmodG, 2)
    cast_weight(w_ab_f, 6)
    cast_weight(w_ab_t, 3)
    cast_weight(w_oab, 2)
    cast_weight(w_fi, 2)
    cast_weight(w_fo, 2)
    cast_weight(w_c_q, 2)
    cast_weight(w_c_kv, 1)
    cast_weight(w_oc, 2)

    PF = K.pool("pf", [128, 512], F32, 4, space="ps")
    PH = K.pool("ph", [128, 512], F32, 2, space="ps")
    PB = K.pool("pb", [128, 1024], BF16, 2, space="ps")

    scf = K.sb("scf", [128, 8, 2], F32)
    K.op("act", lambda e: e.activation(out=scf.t[:], in_=cT_s.t[:], func=AF.Silu), [cT_s()], [scf()])
    scb = K.sb("scb", [128, 8, 2], BF16)
    K.op("dve", lambda e: e.tensor_copy(out=scb.t[:], in_=scf.t[:]), [scf()], [scb()])
    AB1 = K.sb("AB", [128, 4, 8, 2], F32)
    AB = [AB1, AB1]
    Gb1 = [K.sb("Gb_%d" % j, [128, 2048], F32) for j in range(2)]
    Gb = [Gb1, Gb1]
    modA_s = K.sb("modA_s", [128, 32, 2], F32)

    def mod_layer(l):
      with K.scope():
        wmA_pool = K.pool("wmA", [128, 4, 1024], BF16, 2)
        wmG_pool = K.pool("wmG", [128, 8, 512], BF16, 2)
        bG_pool = K.pool("bGs", [128, 512], F32, 2)
        screp = [K.sb("screp%d" % j, [128, 8, 128], BF16) for j in range(2)]
        for j in range(2):
            K.op("dve", lambda e, j=j: e.tensor_copy(out=screp[j].t[:], in_=scf.t[:, :, j:j + 1].to_broadcast([128, 8, 128])),
                 [scf()], [screp[j]()])
        ps = PF.next()
        psv = ps.t[:, 0:64].rearrange("p (b j) -> p b j", j=2)
        for grp in range(8):
            wt = wmA_pool.next()
            K.dma("sp", wt(), WB["wmodA"].v(WB["wmodA"].t[l, grp * 4:(grp + 1) * 4].rearrange("b p k c -> p b (k c)"), key=None))
            for bi in range(4):
                blk = grp * 4 + bi
                for kt in range(8):
                    K.op("pe", lambda e, wt=wt, bi=bi, kt=kt, blk=blk: e.matmul(
                        psv[:, blk, :], lhsT=wt.t[:, bi, kt * 128:(kt + 1) * 128], rhs=scb.t[:, kt, :],
                        start=(kt == 0), stop=(kt == 7)), [wt(), scb()], [ps()])
        K.op("dve", lambda e: e.tensor_tensor(out=modA_s.t[:], in0=psv, in1=bmodA_s.t[:, l, :].unsqueeze(2).to_broadcast([128, 32, 2]), op=ALU.add),
             [ps(), bmodA_s()], [modA_s()])
        mv = modA_s.t[:].rearrange("p (f k) j -> p f k j", f=4)
        ab = AB[l]
        ng = (n1g_s, n2g_s)
        for which in range(2):
            sh = mv[:, 2 * which]
            sc_ = mv[:, 2 * which + 1]
            K.op("dve", lambda e, which=which, sc_=sc_: e.scalar_tensor_tensor(
                out=ab.t[:, 2 * which], in0=sc_, scalar=1.0, in1=ng[which].t[:, l, :].unsqueeze(2).to_broadcast([128, 8, 2]),
                op0=ALU.add, op1=ALU.mult), [modA_s(), ng[which]()], [ab(key=2 * which)])
            K.op("dve", lambda e, which=which, sh=sh: e.tensor_copy(out=ab.t[:, 2 * which + 1], in_=sh), [modA_s()], [ab(key=2 * which + 1)])
        for q in range(4):
            bgs = bG_pool.next()
            K.dma("sp", bgs(), bmodG(np.s_[l, :, q * 512:(q + 1) * 512]))
            wt = wmG_pool.next()
            K.dma("sp", wt(), WB["wmodG"].v(WB["wmodG"].t[l, q], key=None))
            for j in range(2):
                ps2 = PF.next()
                for kt in range(8):
                    K.op("pe", lambda e, wt=wt, kt=kt, j=j, ps2=ps2: e.matmul(
                        ps2.t[:], lhsT=screp[j].t[:, kt, :], rhs=wt.t[:, kt, :], start=(kt == 0), stop=(kt == 7)),
                        [wt(), screp[j]()], [ps2()])
                K.op("dve", lambda e, j=j, q=q, ps2=ps2, bgs=bgs: e.tensor_tensor(
                    out=Gb[l][j].t[:, q * 512:(q + 1) * 512], in0=ps2.t[:], in1=bgs.t[:], op=ALU.add),
                    [ps2(), bgs()], [Gb[l][j](key=q)])

    mod_layer(0)
    dump('AB', AB1); dump('Gb0', Gb1[0]); dump('Gb1', Gb1[1])
    if stage <= 1:
        return K, es, locals()

    xpool = K.pool("xt", [128, D], F32, 2)
    junk = K.pool("junk", [128, D], BF16, 1)
    sspool = K.pool("ss", [128, 2], F32, 4)
    xnpool = K.pool("xn", [128, D], BF16, 4)
    hTpool = K.pool("hT", [128, 8, 512], BF16, 1)

    def rms_rstd(xr, width, out_r):
        jk = junk.next()
        ss = sspool.next()
        K.op("act", lambda e: e.activation(out=jk.t[:, 0:width], in_=xr.ap, func=AF.Square, accum_out=ss.t[:, 0:1]),
             [xr], [jk(), ss()])
        K.op("act", lambda e: e.activation(out=ss.t[:, 1:2], in_=ss.t[:, 0:1], func=AF.Sqrt, scale=1.0 / width, bias=eps_s.t[:, 0:1]),
             [ss(), eps_s()], [ss()])
        K.op("dve", lambda e: e.reciprocal(out=out_r.ap, in_=ss.t[:, 1:2]), [ss()], [out_r])

    eps_s = K.sb("eps_s", [128, 1], F32)
    K.op("dve", lambda e: e.memset(eps_s.t[:], EPS), [], [eps_s()])
    rspool = K.pool("rs", [128, 1], F32, 6)

    def load_x(src, row0):
        xt = xpool.next()
        K.dma("sp", xt(), src(np.s_[row0:row0 + 128, :]))
        return xt

    def norm_mod_T(xtiles, l, which, j):
        n = len(xtiles)
        xns = []
        for xr in xtiles:
            if isinstance(xr, tuple):
                xr = load_x(xr[0], xr[1])()
            rs = rspool.next()
            rms_rstd(xr, D, rs())
            xn = xnpool.next()
            K.op("dve", lambda e, xr=xr, rs=rs, xn=xn: e.tensor_scalar(out=xn.t[:], in0=xr.ap, scalar1=rs.t[:, 0:1], scalar2=None, op0=ALU.mult),
                 [xr, rs()], [xn()])
            xns.append(xn)
        hT = hTpool.next()
        ab = AB[l]
        for kt in range(8):
            pb = PB.next()
            for i, xn in enumerate(xns):
                K.op("pe", lambda e, i=i, xn=xn, pb=pb, kt=kt: e.transpose(out=pb.t[:, i * 128:(i + 1) * 128], in_=xn.t[:, kt * 128:(kt + 1) * 128], identity=identb.t[:]),
                     [xn(), identb()], [pb()])
            K.op("act", lambda e, pb=pb, kt=kt, hT=hT: e.activation(
                out=hT.t[:, kt, 0:n * 128], in_=pb.t[:, 0:n * 128], func=AF.Identity,
                scale=ab.t[:, 2 * which, kt, j:j + 1], bias=ab.t[:, 2 * which + 1, kt, j:j + 1]),
                [pb(), ab(key=2 * which), ab(key=2 * which + 1)], [hT(key=kt)])
        return hT


    K.push_scope()
    ULAT = 16 + 2048 + 16
    UCTX = 16 + 256 + 16
    U = K.sb("U", [128, 4, ULAT + 2 * UCTX], BF16)
    K.op("pool", lambda e: e.memset(U.t[:], 0.0), [], [U()])

    def ucol(tok):
        if tok < 2048:
            return 16 + tok
        c = tok - 2048
        s = c // 256
        return ULAT + s * UCTX + 16 + (c % 256)

    GA = K.sb("GA", [128, NTOK], F32)
    GBt = K.sb("GBt", [128, NTOK], F32)
    K.push_scope()
    wfpool = K.pool("wabf", [128, 8, 128], BF16, 4)
    wtpool = K.pool("wabt", [128, 8, 512], BF16, 2)
    gtmp = K.pool("gtmp", [128, 512], F32, 2)
    evq = K.pool("evq", [128, 512], BF16, 4)
    evt = K.pool("evt", [128, 4, 129], BF16, 3)

    def fm_block(hT, n, wsrc, blk):
        wt = wfpool.next()
        K.dma("sp", wt(), WB[wsrc].v(WB[wsrc].t[blk], key=None))
        ps = PF.next()
        for kt in range(8):
            K.op("pe", lambda e, kt=kt, wt=wt, ps=ps: e.matmul(ps.t[:, 0:n], lhsT=wt.t[:, kt, :], rhs=hT.t[:, kt, 0:n], start=(kt == 0), stop=(kt == 7)),
                 [wt(), hT(key=kt)], [ps()])
        return ps

    def phaseA_group(tiles, j, halo=False):
        n = len(tiles) * 128
        tok0 = tiles[0] * 128
        hT = norm_mod_T([(xs, t * 128) for t in tiles], 0, 0, j)
        for c in range(4):
            psa = fm_block(hT, n, "w_ab_f", c)
            ga = gtmp.next()
            K.op("act", lambda e, psa=psa, ga=ga: e.activation(out=ga.t[:, 0:n], in_=psa.t[:, 0:n], func=AF.Copy), [psa()], [ga()])
            psg = fm_block(hT, n, "w_ab_f", 4 + c)
            sg = gtmp.next()
            K.op("act", lambda e, psg=psg, sg=sg: e.activation(out=sg.t[:, 0:n], in_=psg.t[:, 0:n], func=AF.Sigmoid), [psg()], [sg()])
            if halo:
                c0 = 16 + 2048
                segs = [(0, 16, c0)]
            elif tok0 < 2048:
                segs = [(0, n, ucol(tok0))]
            else:
                segs = [(0, 256, ucol(2048)), (256, 256, ucol(2048 + 256))]
            for (o0, ln, uc) in segs:
                K.op("dve", lambda e, ga=ga, sg=sg, c=c, o0=o0, ln=ln, uc=uc: e.tensor_tensor(
                    out=U.t[:, c, uc:uc + ln], in0=ga.t[:, o0:o0 + ln], in1=sg.t[:, o0:o0 + ln], op=ALU.mult),
                    [ga(), sg()], [U(key=(c, uc))])
        if halo:
            return
        for (dst, b0, scl) in ((S_qT, 8, 1.0), (S_kT, 12, 128 ** -0.5)):
            for h in range(4):
                ps = fm_block(hT, n, "w_ab_f", b0 + h)
                ev = evq.next()
                K.op("act", lambda e, ps=ps, ev=ev, scl=scl: e.activation(out=ev.t[:, 0:n], in_=ps.t[:, 0:n], func=AF.Copy, scale=scl), [ps()], [ev()])
                K.dma("pool", dst(np.s_[:, h, tok0:tok0 + n], key=tok0), ev(np.s_[:, 0:n]))
        for (dst, blk, col) in ((GA, 16, 0), (GBt, 17, 1)):
            ps = fm_block(hT, n, "w_ab_f", blk)
            K.op("act", lambda e, ps=ps, dst=dst, col=col: e.activation(out=dst.t[:, tok0:tok0 + n], in_=ps.t[:, 0:n], func=AF.Identity, bias=bg_s.t[:, col:col + 1]),
                 [ps(), bg_s()], [dst(key=tok0)])
        for wi, dst in enumerate((S_k, S_v, S_o)):
            wt = wtpool.next()
            K.dma("sp", wt(), WB["w_ab_t"].v(WB["w_ab_t"].t[wi], key=None))
            for i, t in enumerate(tiles):
                ps = PF.next()
                for kt in range(8):
                    K.op("pe", lambda e, kt=kt, wt=wt, ps=ps, i=i: e.matmul(ps.t[:], lhsT=hT.t[:, kt, i * 128:(i + 1) * 128], rhs=wt.t[:, kt, :], start=(kt == 0), stop=(kt == 7)),
                         [wt(), hT(key=kt)], [ps()])
                if wi == 1:
                    ev = evt.next()
                    K.op("dve", lambda e, ev=ev: e.memset(ev.t[:, :, 128:129], 1.0), [], [ev()])
                    K.op("act", lambda e, ps=ps, ev=ev: e.activation(out=ev.t[:, :, 0:128], in_=ps.t[:].rearrange("p (h c) -> p h c", h=4), func=AF.Copy), [ps()], [ev()])
                    K.dma("pool", dst(np.s_[t * 128:(t + 1) * 128], key=t), ev())
                else:
                    ev = evq.next()
                    K.op("act", lambda e, ps=ps, ev=ev, wi=wi: e.activation(out=ev.t[:], in_=ps.t[:], func=AF.Copy, scale=(128 ** -0.5 if wi == 0 else 1.0)), [ps()], [ev()])
                    K.dma("pool", dst(np.s_[t * 128:(t + 1) * 128, :], key=t), ev())

    for g in range(5):
        phaseA_group(list(range(4 * g, 4 * g + 4)), 0 if g < 4 else 1)
    phaseA_group([20], 0, halo=True)
    K.pop_scope()
    dump('U', U); dump('GA', GA); dump('GBt', GBt)
    if stage <= 2:
        K.pop_scope()
        return K, es, locals()


    K.push_scope()
    Dg = K.sb("Dg", [128, 4, 31, 128], BF16)
    for c in range(4):
        for jj in range(31):
            K.op("dve", lambda e, c=c, jj=jj: e.tensor_scalar(out=Dg.t[:, c, jj, :], in0=identf.t[:], scalar1=cw_s.t[:, c, jj:jj + 1], scalar2=None, op0=ALU.mult),
                 [identf(), cw_s()], [Dg(key=(c, jj))])
    ucp = K.pool("ucp", [128, 512], F32, 5)
    lnp = K.pool("lnp", [128, 512], F32, 12)
    aop = K.pool("aop", [128, 512], BF16, 3)

    def conv_group(tok0):
        if tok0 < 2048:
            segs = [(0, 512, ucol(tok0))]
        else:
            segs = [(0, 256, ucol(2048)), (256, 256, ucol(2048 + 256))]
        ucs = []
        for c in range(4):
            ps = PF.next()
            for (o0, ln, uc) in segs:
                for jj in range(31):
                    K.op("pe", lambda e, c=c, jj=jj, ps=ps, o0=o0, ln=ln, uc=uc: e.matmul(
                        ps.t[:, o0:o0 + ln], lhsT=Dg.t[:, c, jj, :], rhs=U.t[:, c, uc + jj - 15:uc + jj - 15 + ln], start=(jj == 0), stop=(jj == 30)),
                        [Dg(), U()], [ps()])
            ucb = ucp.next()
            K.op("act", lambda e, ps=ps, ucb=ucb, c=c: e.activation(out=ucb.t[:], in_=ps.t[:], func=AF.Identity, bias=cb_s.t[:, c:c + 1]), [ps(), cb_s()], [ucb()])
            ucs.append(ucb)
        ps_s = PF.next()
        ps_q = PF.next()
        for c in range(4):
            K.op("pe", lambda e, c=c: e.matmul(ps_s.t[:], lhsT=onesf.t[:], rhs=ucs[c].t[:], start=(c == 0), stop=(c == 3)), [onesf(), ucs[c]()], [ps_s()])
        for c in range(4):
            sq = lnp.next()
            K.op("act", lambda e, c=c, sq=sq: e.activation(out=sq.t[:], in_=ucs[c].t[:], func=AF.Square), [ucs[c]()], [sq()])
            K.op("pe", lambda e, c=c, sq=sq: e.matmul(ps_q.t[:], lhsT=onesf.t[:], rhs=sq.t[:], start=(c == 0), stop=(c == 3)), [onesf(), sq()], [ps_q()])
        mean = lnp.next()
        K.op("act", lambda e: e.activation(out=mean.t[:], in_=ps_s.t[:], func=AF.Copy, scale=1.0 / 512), [ps_s()], [mean()])
        msq = lnp.next()
        K.op("dve", lambda e: e.tensor_tensor(out=msq.t[:], in0=mean.t[:], in1=mean.t[:], op=ALU.mult), [mean()], [msq()])
        var = lnp.next()
        K.op("dve", lambda e: e.scalar_tensor_tensor(out=var.t[:], in0=ps_q.t[:], scalar=1.0 / 512, in1=msq.t[:], op0=ALU.mult, op1=ALU.subtract), [ps_q(), msq()], [var()])
        K.op("act", lambda e: e.activation(out=var.t[:], in_=var.t[:], func=AF.Sqrt, bias=eps_s.t[:, 0:1]), [var(), eps_s()], [var()])
        rstd = lnp.next()
        K.op("dve", lambda e: e.reciprocal(out=rstd.t[:], in_=var.t[:]), [var()], [rstd()])
        for c in range(4):
            t1 = lnp.next()
            K.op("dve", lambda e, c=c, t1=t1: e.tensor_tensor(out=t1.t[:], in0=ucs[c].t[:], in1=mean.t[:], op=ALU.subtract), [ucs[c](), mean()], [t1()])
            K.op("dve", lambda e, t1=t1: e.tensor_tensor(out=t1.t[:], in0=t1.t[:], in1=rstd.t[:], op=ALU.mult), [t1(), rstd()], [t1()])
            ao = aop.next()
            K.op("act", lambda e, c=c, t1=t1, ao=ao: e.activation(out=ao.t[:], in_=t1.t[:], func=AF.Silu, scale=lng_s.t[:, c:c + 1], bias=lnb_s.t[:, c:c + 1]),
                 [t1(), lng_s(), lnb_s()], [ao()])
            K.dma("pool", S_act(np.s_[:, c, tok0:tok0 + 512], key=("u", tok0)), ao())

    for g in range(5):
        conv_group(g * 512)
    K.pop_scope()
    if stage <= 3:
        K.pop_scope()
        return K, es, locals()

    K.push_scope()
    NCH = 40
    rmask = K.sb("rmask", [4, NTOK], F32)
    K.op("dve", lambda e: e.memset(rmask.t[:], 1.0), [], [rmask()])
    K.op("dve", lambda e: e.memset(rmask.t[:].rearrange("p (c t) -> p c t", t=64)[:, :, 0:1], 0.0), [], [rmask()])
    gp = {n_: K.sb("gp_" + n_, [4, NTOK], F32) for n_ in ("lp", "cs", "bn", "a")}
    gp["wk"] = gp["a"]
    gp["e"] = gp["bn"]
    gs = {n_: K.sb("gs_" + n_, [4, NCH], F32) for n_ in ("amax", "Mc", "m", "mprev", "g", "tmp")}
    Gd = K.sb("Gd", [4, 4, NCH], F32)
    WE = K.sb("WE", [128, NTT, 8], F32)
    gB = K.sb("gB", [128, 4, NCH], F32)
    zero4 = K.sb("zero4", [128, 4], F32)
    K.op("dve", lambda e: e.memset(zero4.t[:], 0.0), [], [zero4()])
    mlat = K.sb("mlat", [4, 1], F32)
    K.dma("sp", mlat(), st_m())
    Cext = K.sb("Cext", [128, 4, 129], F32)
    Cgp = K.pool("Cg", [128, 4, 129], BF16, 3)
    qzp = [(K.sb("qz0_%d" % i, [128, 4, 128], BF16), K.sb("qz1_%d" % i, [128, 4, 128], BF16)) for i in range(2)]
    for (a_, b_) in qzp:
        K.op("dve", lambda e, a_=a_: e.memset(a_.t[:], 0.0), [], [a_()])
        K.op("dve", lambda e, b_=b_: e.memset(b_.t[:], 0.0), [], [b_()])
    kTp = K.pool("kTt", [128, 4, 128], BF16, 2)
    ktp = K.pool("ktt", [128, 512], BF16, 2)
    vtp = K.pool("vtt", [128, 4, 129], BF16, 2)
    Smp = K.pool("Sm", [128, 4, 128], BF16, 2)
    Vpp = K.pool("Vp", [128, 4, 129], BF16, 2)
    hhp = K.pool("hh", [128, 4, 128], F32, 2)
    s4p = K.pool("s4", [128, 8], F32, 4)
    otp = K.pool("ot", [128, 512], BF16, 2)
    hfp = K.pool("hft", [128, 512], F32, 2)
    hmp = K.pool("hm", [128, 512], BF16, 2)
    hmT = K.pool("hmT", [128, 4, 128], BF16, 2)
    stg = K.pool("stg", [128, 520], F32, 2)

    groups = [(list(range(0, 16)), "lat"), ([16, 17], 0), ([18, 19], 1)]

    def gates_prep(G, pas):
        i_ = G.t[0:4, :]
        f_ = G.t[32:36, :]
        lp, cs, bn, a, wk, ee = (gp[n_] for n_ in ("lp", "cs", "bn", "a", "wk", "e"))
        K.op("act", lambda e: e.activation(out=lp.t[:], in_=f_, func=AF.Exp, scale=-1.0), [G()], [lp()])
        K.op("act", lambda e: e.activation(out=lp.t[:], in_=lp.t[:], func=AF.Ln, bias=onesf.t[0:4, 0:1]), [lp(), onesf()], [lp()])
        K.op("dve", lambda e: e.tensor_tensor_scan(out=cs.t[:], data0=rmask.t[:], data1=lp.t[:], initial=0.0, op0=ALU.mult, op1=ALU.add), [rmask(), lp()], [cs()])
        c3 = lambda b_: b_.t[:].rearrange("p (c t) -> p c t", t=64)
        if pas == 0:
            K.op("dve", lambda e: e.tensor_copy(out=bn.t[:], in_=cs.t[:]), [cs()], [bn()])
            bend = c3(cs)[:, :, 63]
        else:
            K.op("dve", lambda e: e.tensor_tensor(out=c3(bn), in0=c3(lp), in1=c3(cs), op=ALU.subtract), [lp(), cs()], [bn()])
            K.op("dve", lambda e: e.tensor_tensor(out=c3(bn), in0=c3(bn), in1=c3(cs)[:, :, 63:64].to_broadcast([4, NCH, 64]), op=ALU.add), [bn(), cs()], [bn()])
            K.op("dve", lambda e: e.tensor_copy(out=cs.t[:], in_=bn.t[:]), [bn()], [cs()])
            bend = c3(cs)[:, :, 0]
        K.op("dve", lambda e: e.tensor_tensor(out=a.t[:], in0=i_, in1=bn.t[:], op=ALU.add), [G(), bn()], [a()])
        K.op("dve", lambda e: e.tensor_reduce(out=gs["amax"].t[:], in_=c3(a), axis=AX.X, op=ALU.max), [a()], [gs["amax"]()])
        for (tiles, kind) in groups:
            chs = list(range(tiles[0] * 2, tiles[-1] * 2 + 2))
            if pas == 1:
                chs = chs[::-1]
            for ci, c in enumerate(chs):
                if ci == 0:
                    prev = mlat.t[:, 0:1] if kind == "lat" else zero4.t[0:4, 0:1]
                    prd = [mlat()] if kind == "lat" else [zero4()]
                else:
                    pc = chs[ci - 1]
                    prev = gs["m"].t[:, pc:pc + 1]
                    prd = [gs["m"]()]
                K.op("dve", lambda e, c=c, prev=prev: e.tensor_copy(out=gs["mprev"].t[:, c:c + 1], in_=prev), prd, [gs["mprev"]()])
                K.op("dve", lambda e, c=c, prev=prev: e.tensor_tensor(out=gs["Mc"].t[:, c:c + 1], in0=prev, in1=gs["amax"].t[:, c:c + 1], op=ALU.max), prd + [gs["amax"]()], [gs["Mc"]()])
                K.op("dve", lambda e, c=c: e.tensor_tensor(out=gs["m"].t[:, c:c + 1], in0=gs["Mc"].t[:, c:c + 1], in1=bend[:, c:c + 1], op=ALU.subtract), [gs["Mc"](), cs()], [gs["m"]()])
        K.op("dve", lambda e: e.tensor_tensor(out=gs["tmp"].t[:], in0=gs["mprev"].t[:], in1=gs["Mc"].t[:], op=ALU.subtract), [gs["mprev"](), gs["Mc"]()], [gs["tmp"]()])
        K.op("act", lambda e: e.activation(out=gs["g"].t[:], in_=gs["tmp"].t[:], func=AF.Exp), [gs["tmp"]()], [gs["g"]()])
        mcb = gs["Mc"].t[:].unsqueeze(2).to_broadcast([4, NCH, 64])
        K.op("dve", lambda e: e.tensor_tensor(out=c3(wk), in0=c3(a), in1=mcb, op=ALU.subtract), [a(), gs["Mc"]()], [wk()])
        K.op("act", lambda e: e.activation(out=wk.t[:], in_=wk.t[:], func=AF.Exp), [wk()], [wk()])
        K.op("dve", lambda e: e.tensor_tensor(out=c3(ee), in0=c3(bn), in1=mcb, op=ALU.subtract), [bn(), gs["Mc"]()], [ee()])
        K.op("act", lambda e: e.activation(out=ee.t[:], in_=ee.t[:], func=AF.Exp), [ee()], [ee()])
        ps = PF.next()
        psv = ps.t[:, 0:NTT * 8].rearrange("p (t k) -> p t k", k=8)
        for t in range(NTT):
            K.op("pe", lambda e, t=t: e.matmul(psv[:, t, 0:4], lhsT=wk.t[:, t * 128:(t + 1) * 128], rhs=identf.t[0:4, 0:4], start=True, stop=True), [wk(), identf()], [ps()])
            K.op("pe", lambda e, t=t: e.matmul(psv[:, t, 4:8], lhsT=ee.t[:, t * 128:(t + 1) * 128], rhs=identf.t[0:4, 0:4], start=True, stop=True), [ee(), identf()], [ps()])
        K.op("dve", lambda e: e.tensor_copy(out=WE.t[:], in_=psv), [ps()], [WE()])
        K.op("dve", lambda e: e.tensor_tensor(out=Gd.t[:], in0=identf.t[0:4, 0:4].unsqueeze(2).to_broadcast([4, 4, NCH]),
                                              in1=gs["g"].t[:].unsqueeze(1).to_broadcast([4, 4, NCH]), op=ALU.mult), [identf(), gs["g"]()], [Gd()])
        ps2 = PF.next()
        K.op("pe", lambda e: e.matmul(ps2.t[:, 0:4 * NCH], lhsT=onesf.t[0:4, :], rhs=Gd.t[:].rearrange("p a c -> p (a c)"), start=True, stop=True), [onesf(), Gd()], [ps2()])
        K.op("dve", lambda e: e.tensor_copy(out=gB.t[:].rearrange("p a c -> p (a c)"), in_=ps2.t[:, 0:4 * NCH]), [ps2()], [gB()])

    def mlstm_pass(pas):
        mask = maskA if pas == 0 else maskB
        tcount = 0
        for (tiles, kind) in groups:
            order = tiles if pas == 0 else tiles[::-1]
            if kind == "lat":
                if pas == 0:
                    K.dma("sp", Cext(), st_C())
            else:
                K.op("dve", lambda e: e.memset(Cext.t[:], 0.0), [], [Cext()])
            fc = order[0] * 2 + (0 if pas == 0 else 1)
            Cg = Cgp.next()
            K.op("dve", lambda e, Cg=Cg, fc=fc: e.tensor_tensor(out=Cg.t[:], in0=Cext.t[:], in1=gB.t[:, :, fc:fc + 1].to_broadcast([128, 4, 129]), op=ALU.mult), [Cext(), gB()], [Cg()])
            for ti, t in enumerate(order):
                tok0 = t * 128
                qz0, qz1 = qzp[tcount % 2]
                tcount += 1
                K.dma("sp", qz0(np.s_[:, :, 0:64]), S_qT(np.s_[:, :, tok0:tok0 + 64]))
                K.dma("sp", qz1(np.s_[:, :, 64:128]), S_qT(np.s_[:, :, tok0 + 64:tok0 + 128]))
                kT = kTp.next(); K.dma("sp", kT(), S_kT(np.s_[:, :, tok0:tok0 + 128]))
                kt_ = ktp.next(); K.dma("sp", kt_(), S_k(np.s_[tok0:tok0 + 128, :]))
                vt = vtp.next(); K.dma("sp", vt(), S_v(np.s_[tok0:tok0 + 128]))
                psS = PF.next()
                sv = psS.t[:].rearrange("p (h t) -> p h t", h=4)
                for h in range(4):
                    K.op("pe", lambda e, h=h, kT=kT, qz0=qz0: e.matmul(sv[:, h, 0:64], lhsT=kT.t[:, h, :], rhs=qz0.t[:, h, 0:64], start=True, stop=True), [kT(), qz0()], [psS()])
                    K.op("pe", lambda e, h=h, kT=kT, qz1=qz1: e.matmul(sv[:, h, 64:128], lhsT=kT.t[:, h, :], rhs=qz1.t[:, h, 64:128], start=True, stop=True), [kT(), qz1()], [psS()])
                Sm = Smp.next()
                K.op("dve", lambda e, Sm=Sm, sv=sv: e.tensor_tensor(out=Sm.t[:], in0=sv, in1=mask.t[:].unsqueeze(1).to_broadcast([128, 4, 128]), op=ALU.mult), [psS(), mask()], [Sm()])
                Vp = Vpp.next()
                K.op("dve", lambda e, Vp=Vp, vt=vt, t=t: e.tensor_tensor(out=Vp.t[:], in0=vt.t[:], in1=WE.t[:, t, 0:4].unsqueeze(2).to_broadcast([128, 4, 129]), op=ALU.mult), [vt(), WE()], [Vp()])
                nd = [PH.next(), PH.next()]
                ndv = [b_.t[:, 0:258].rearrange("p (h c) -> p h c", h=2) for b_ in nd]
                halves = [(0, 64), (64, 128)] if pas == 0 else [(64, 128), (0, 64)]
                qzs = [qz0, qz1] if pas == 0 else [qz1, qz0]
                for h in range(4):
                    K.op("pe", lambda e, h=h, Sm=Sm, Vp=Vp: e.matmul(ndv[h // 2][:, h % 2, :], lhsT=Sm.t[:, h, :], rhs=Vp.t[:, h, :], start=(h % 2 == 0), stop=False), [Sm(), Vp()], [nd[h // 2]()])
                for half in range(2):
                    r0, r1 = halves[half]
                    cidx = t * 2 + (0 if r0 == 0 else 1)
                    qz = qzs[half]
                    for h in range(4):
                        K.op("pe", lambda e, h=h, qz=qz, Cg=Cg, half=half: e.matmul(ndv[h // 2][:, h % 2, :], lhsT=qz.t[:, h, :], rhs=Cg.t[:, h, :], start=False, stop=(half == 1 and h % 2 == 1)), [qz(), Cg()], [nd[h // 2]()])
                    dC = [PF.next(), PF.next()]
                    dCv = [b_.t[:, 0:258].rearrange("p (h c) -> p h c", h=2) for b_ in dC]
                    for h in range(4):
                        K.op("pe", lambda e, h=h, kt_=kt_, Vp=Vp, r0=r0, r1=r1, dCv=dCv: e.matmul(dCv[h // 2][:, h % 2, :], lhsT=kt_.t[r0:r1, h * 128:(h + 1) * 128], rhs=Vp.t[r0:r1, h, :], start=True, stop=True), [kt_(), Vp()], [dC[h // 2]()])
                    for h in range(4):
                        K.op("dve", lambda e, h=h, cidx=cidx, dCv=dCv: e.scalar_tensor_tensor(out=Cext.t[:, h, :], in0=Cext.t[:, h, :], scalar=gB.t[:, h, cidx:cidx + 1], in1=dCv[h // 2][:, h % 2, :], op0=ALU.mult, op1=ALU.add),
                             [Cext(), gB(), dC[h // 2]()], [Cext()])
                    if half == 0:
                        nxt = t * 2 + (1 if r0 == 0 else 0)
                    elif ti + 1 < len(order):
                        nxt = order[ti + 1] * 2 + (0 if pas == 0 else 1)
                    else:
                        nxt = None
                    if nxt is not None:
                        Cg = Cgp.next()
                        K.op("dve", lambda e, Cg=Cg, nxt=nxt: e.tensor_tensor(out=Cg.t[:], in0=Cext.t[:], in1=gB.t[:, :, nxt:nxt + 1].to_broadcast([128, 4, 129]), op=ALU.mult), [Cext(), gB()], [Cg()])
                s4 = s4p.next()
                for b2 in range(2):
                    K.op("act", lambda e, b2=b2, s4=s4: e.activation(out=s4.t[:, 2 * b2:2 * b2 + 2], in_=ndv[b2][:, :, 128], func=AF.Abs), [nd[b2]()], [s4()])
                    K.op("dve", lambda e, b2=b2, s4=s4, t=t: e.tensor_tensor(out=s4.t[:, 2 * b2:2 * b2 + 2], in0=s4.t[:, 2 * b2:2 * b2 + 2], in1=WE.t[:, t, 4 + 2 * b2:6 + 2 * b2], op=ALU.max),
                         [s4(), WE()], [s4()])
                K.op("dve", lambda e, s4=s4: e.reciprocal(out=s4.t[:, 4:8], in_=s4.t[:, 0:4]), [s4()], [s4()])
                hh = hhp.next()
                for b2 in range(2):
                    K.op("dve", lambda e, b2=b2, s4=s4, hh=hh: e.tensor_tensor(out=hh.t[:, 2 * b2:2 * b2 + 2, :], in0=ndv[b2][:, :, 0:128], in1=s4.t[:, 4 + 2 * b2:6 + 2 * b2].unsqueeze(2).to_broadcast([128, 2, 128]), op=ALU.mult),
                         [nd[b2](), s4()], [hh()])
                if pas == 0:
                    K.dma("pool", S_hf(np.s_[tok0:tok0 + 128, :], key=t), hh.v(hh.t[:].rearrange("p h c -> p (h c)")))
                else:
                    hf = hfp.next(); K.dma("sp", hf(), S_hf(np.s_[tok0:tok0 + 128, :], key=t))
                    ot = otp.next(); K.dma("sp", ot(), S_o(np.s_[tok0:tok0 + 128, :], key=t))
                    hs = hh.t[:].rearrange("p h c -> p (h c)")
                    K.op("dve", lambda e, hs=hs, hf=hf: e.tensor_tensor(out=hs, in0=hs, in1=hf.t[:], op=ALU.add), [hh(), hf()], [hh()])
                    K.op("dve", lambda e, hs=hs, hf=hf: e.tensor_tensor(out=hf.t[:], in0=hs, in1=hs, op=ALU.mult), [hh()], [hf()])
                    s5 = s4p.next()
                    K.op("dve", lambda e, s5=s5, hf=hf: e.tensor_reduce(out=s5.t[:, 0:4], in_=hf.t[:].rearrange("p (h c) -> p h c", h=4), axis=AX.X, op=ALU.add), [hf()], [s5()])
                    K.op("act", lambda e, s5=s5: e.activation(out=s5.t[:, 0:4], in_=s5.t[:, 0:4], func=AF.Sqrt, scale=1.0 / 128, bias=eps_s.t[:, 0:1]), [s5(), eps_s()], [s5()])
                    K.op("dve", lambda e, s5=s5: e.reciprocal(out=s5.t[:, 4:8], in_=s5.t[:, 0:4]), [s5()], [s5()])
                    K.op("dve", lambda e, s5=s5, hh=hh: e.tensor_tensor(out=hh.t[:], in0=hh.t[:], in1=s5.t[:, 4:8].unsqueeze(2).to_broadcast([128, 4, 128]), op=ALU.mult), [hh(), s5()], [hh()])
                    K.op("dve", lambda e, hs=hs: e.tensor_tensor(out=hs, in0=hs, in1=mng_s.t[:], op=ALU.mult), [hh(), mng_s()], [hh()])
                    K.op("act", lambda e, hf=hf, ot=ot: e.activation(out=hf.t[:], in_=ot.t[:], func=AF.Sigmoid), [ot()], [hf()])
                    hm = hmp.next()
                    K.op("dve", lambda e, hs=hs, hf=hf, hm=hm: e.tensor_tensor(out=hm.t[:], in0=hs, in1=hf.t[:], op=ALU.mult), [hh(), hf()], [hm()])
                    pb = PB.next()
                    for h in range(4):
                        K.op("pe", lambda e, h=h, hm=hm, pb=pb: e.transpose(out=pb.t[:, h * 128:(h + 1) * 128], in_=hm.t[:, h * 128:(h + 1) * 128], identity=identb.t[:]), [hm(), identb()], [pb()])
                    hT_ = hmT.next()
                    K.op("act", lambda e, pb=pb, hT_=hT_: e.activation(out=hT_.t[:].rearrange("p h c -> p (h c)"), in_=pb.t[:, 0:512], func=AF.Copy), [pb()], [hT_()])
                    K.dma("pool", S_act(np.s_[:, 4:8, tok0:tok0 + 128], key=("h", t)), hT_())
            lastc = order[-1] * 2 + (1 if pas == 0 else 0)
            if kind == "lat":
                if pas == 0:
                    K.dma("pool", S_st(np.s_[:, 516:520], key="z"), zero4())
                    K.dma("pool", S_st(np.s_[:, 0:516], key="c"), Cext.v(Cext.t[:].rearrange("p h c -> p (h c)")))
                    K.dma("pool", S_st(np.s_[0:4, 516:517], key="z"), gs["m"](np.s_[:, lastc:lastc + 1]), allow_slow_non_contiguous=True)
            else:
                sq = kind
                K.dma("pool", oC.v(oC.t[sq, pas].rearrange("h d v -> d h v")), Cext(np.s_[:, :, 0:128]))
                K.dma("pool", on.v(on.t[sq, pas].rearrange("h (d o) -> d h o", o=1)), Cext(np.s_[:, :, 128:129]), allow_slow_non_contiguous=True)
                K.dma("pool", om.v(om.t[sq, pas].rearrange("(h o) -> h o", o=1)), gs["m"](np.s_[:, lastc:lastc + 1]), allow_slow_non_contiguous=True)

    gates_prep(GA, 0)
    dump('WE', WE); dump('gB', gB)
    if stage <= 4:
        K.pop_scope(); K.pop_scope()
        return K, es, locals()
    mlstm_pass(0)
    if stage <= 5:
        K.pop_scope(); K.pop_scope()
        return K, es, locals()
    K.cc(lambda e: e.collective_compute("AllGather", ALU.bypass, replica_groups=[[0, 1], [2, 3], [4, 5], [6, 7]], ins=[S_st.t[:, :]], outs=[S_stg.t[:, :]]),
         [S_st()], [S_stg()])
    g0 = stg.next(); K.dma("sp", g0(), S_stg(np.s_[0:128, :]))
    g1 = stg.next(); K.dma("sp", g1(), S_stg(np.s_[128:256, :]))
    K.op("dve", lambda e: e.tensor_scalar(out=g1.t[:], in0=g1.t[:], scalar1=sel_s.t[:, 1:2], scalar2=None, op0=ALU.mult), [g1(), sel_s()], [g1()])
    K.op("dve", lambda e: e.scalar_tensor_tensor(out=g0.t[:], in0=g0.t[:], scalar=sel_s.t[:, 0:1], in1=g1.t[:], op0=ALU.mult, op1=ALU.add), [g0(), g1(), sel_s()], [g0()])
    K.op("dve", lambda e: e.tensor_copy(out=Cext.t[:].rearrange("p h c -> p (h c)"), in_=g0.t[:, 0:516]), [g0()], [Cext()])
    K.op("dve", lambda e: e.tensor_copy(out=mlat.t[:], in_=g0.t[0:4, 516:517]), [g0()], [mlat()])
    if stage <= 6:
        K.pop_scope(); K.pop_scope()
        return K, es, locals()
    gates_prep(GBt, 1)
    mlstm_pass(1)
    K.pop_scope()
    K.pop_scope()

    def post_scope_alloc(need_aT=True):
        P = {}
        P["wo"] = [K.sb("wo%d" % i, [128, 8, 512], BF16) for i in range(2)]
        if need_aT:
            P["aT"] = K.pool("aT", [128, 8, 512], BF16, 2)
        P["x1"] = K.pool("x1", [128, D], F32, 4)
        P["tmp"] = K.pool("ptmp", [128, 512], F32, 2)
        P["wfi"] = K.pool("wfi", [128, 8, 256], BF16, 2)
        P["wfo"] = K.pool("wfo", [128, 2, 512], BF16, 3)
        P["actF"] = K.pool("actF", [128, NFB, 512], BF16, 1)
        P["sg"] = K.pool("sg", [128, 512], F32, 1)
        return P

    def load_wo(P, name):
        for half in range(2):
            K.dma("sp", P["wo"][half](), WB[name].v(WB[name].t[half]))

    def outproj_residual(P, l, j, aT, i, xin):
        x1 = P["x1"].next()
        for half in range(2):
            ps = PF.next()
            for kt in range(8):
                K.op("pe", lambda e, kt=kt, ps=ps, half=half: e.matmul(ps.t[:], lhsT=aT.t[:, kt, i * 128:(i + 1) * 128], rhs=P["wo"][half].t[:, kt, :], start=(kt == 0), stop=(kt == 7)),
                     [aT(), P["wo"][half]()], [ps()])
            tm = P["tmp"].next()
            K.op("dve", lambda e, ps=ps, tm=tm, half=half: e.tensor_tensor(out=tm.t[:], in0=ps.t[:], in1=Gb[l][j].t[:, half * 512:(half + 1) * 512], op=ALU.mult), [ps(), Gb[l][j]()], [tm()])
            K.op("dve", lambda e, tm=tm, half=half, x1=x1: e.tensor_tensor(out=x1.t[:, half * 512:(half + 1) * 512], in0=tm.t[:], in1=xin.ap[:, half * 512:(half + 1) * 512], op=ALU.add), [tm(), xin], [x1(key=half)])
        return x1

    def ffn_group(P, l, j, x1s, tok0, final):
        hT2 = norm_mod_T([x() for x in x1s], l, 1, j)
        actF = P["actF"].next()
        for jb in range(NFB):
            wt = P["wfi"].next()
            K.dma("sp", wt(), WB["w_fi"].v(WB["w_fi"].t[l, jb]))
            psg = PF.next()
            psu = PF.next()
            for kt in range(8):
                K.op("pe", lambda e, kt=kt, wt=wt, psg=psg: e.matmul(psg.t[:], lhsT=wt.t[:, kt, 0:128], rhs=hT2.t[:, kt, :], start=(kt == 0), stop=(kt == 7)), [wt(), hT2(key=kt)], [psg()])
            for kt in range(8):
                K.op("pe", lambda e, kt=kt, wt=wt, psu=psu: e.matmul(psu.t[:], lhsT=wt.t[:, kt, 128:256], rhs=hT2.t[:, kt, :], start=(kt == 0), stop=(kt == 7)), [wt(), hT2(key=kt)], [psu()])
            sg = P["sg"].next()
            K.op("act", lambda e, psg=psg, sg=sg: e.activation(out=sg.t[:], in_=psg.t[:], func=AF.Silu), [psg()], [sg()])
            K.op("dve", lambda e, psu=psu, sg=sg, jb=jb: e.tensor_tensor(out=actF.t[:, jb, :], in0=psu.t[:], in1=sg.t[:], op=ALU.mult), [psu(), sg()], [actF(key=jb)])
        for half in range(2):
            pss = [PH.next(), PH.next(), PF.next(), PF.next()]
            for jc in range(11):
                wt = P["wfo"].next()
                K.dma("sp", wt(), WB["w_fo"].v(WB["w_fo"].t[l, half, :, jc * 2:(jc + 1) * 2, :]))
                for i in range(4):
                    for jj in range(2):
                        jb = jc * 2 + jj
                        K.op("pe", lambda e, i=i, jj=jj, jb=jb, wt=wt: e.matmul(pss[i].t[:], lhsT=actF.t[:, jb, i * 128:(i + 1) * 128], rhs=wt.t[:, jj, :], start=(jb == 0), stop=(jb == NFB - 1)),
                             [actF(key=jb), wt()], [pss[i]()])
            for i in range(4):
                tm = P["tmp"].next()
                K.op("dve", lambda e, i=i, tm=tm, half=half: e.tensor_tensor(out=tm.t[:], in0=pss[i].t[:], in1=Gb[l][j].t[:, 1024 + half * 512:1024 + (half + 1) * 512], op=ALU.mult), [pss[i](), Gb[l][j]()], [tm()])
                K.op("dve", lambda e, i=i, tm=tm, half=half: e.tensor_tensor(out=x1s[i].t[:, half * 512:(half + 1) * 512], in0=tm.t[:], in1=x1s[i].t[:, half * 512:(half + 1) * 512], op=ALU.add), [tm(), x1s[i](key=half)], [x1s[i](key=half)])
        for i in range(4):
            r0 = tok0 + i * 128
            if not final:
                K.dma("pool", S_x1(np.s_[r0:r0 + 128, :], key=r0), x1s[i]())
            else:
                rs = rspool.next()
                rms_rstd(x1s[i](), D, rs())
                K.op("dve", lambda e, i=i, rs=rs: e.scalar_tensor_tensor(out=x1s[i].t[:], in0=x1s[i].t[:], scalar=rs.t[:, 0:1], in1=fng_s.t[:], op0=ALU.mult, op1=ALU.mult), [x1s[i](), rs(), fng_s()], [x1s[i]()])
                K.dma("pool", y(np.s_[r0:r0 + 128, :]), x1s[i]())

    if stage <= 7:
        return K, es, locals()
    K.push_scope()
    P = post_scope_alloc()
    load_wo(P, "w_oab")
    for g in range(5):
        j = 0 if g < 4 else 1
        tok0 = g * 512
        aT = P["aT"].next()
        K.dma("sp", aT(), S_act(np.s_[:, :, tok0:tok0 + 512]))
        x1s = []
        for i in range(4):
            xt = load_x(xs, tok0 + i * 128)
            x1s.append(outproj_residual(P, 0, j, aT, i, xt()))
        ffn_group(P, 0, j, x1s, tok0, final=False)
    K.pop_scope()

    if stage <= 8:
        return K, es, locals()
    mod_layer(1)
    S_ckT = K.dram("S_ckT", [128, 2, 512], BF16)
    S_cv = K.dram("S_cv", [512, 2, 128], BF16)

    def head_norm(src3, nh, gain, out3, P):
        sq = P["hn_sq"].next()
        sqv = sq.t[:, 0:nh * 128].rearrange("p (h c) -> p h c", h=nh)
        s8 = P["hn_s"].next()
        return sq, sqv, s8

    def rope(P, xq, nh, t, out_bf):
        xv = xq.t[:, 0:nh * 128].rearrange("p (h r a c) -> p h r a c", h=nh, r=2, a=2)
        ov = out_bf.t[:, 0:nh * 128].rearrange("p (h r a c) -> p h r a c", h=nh, r=2, a=2)
        A_ = P["ropeA"].next()
        B_ = P["ropeB"].next()
        av = A_.t[:, 0:nh * 128].rearrange("p (h r a c) -> p h r a c", h=nh, r=2, a=2)
        bv = B_.t[:, 0:nh * 128].rearrange("p (h r a c) -> p h r a c", h=nh, r=2, a=2)
        cosb = rcos_s.t[:, t, :].rearrange("p (r c) -> p r c", r=2)
        sinb = rsin_s.t[:, t, :].rearrange("p (r c) -> p r c", r=2)
        for a in range(2):
            K.op("dve", lambda e, a=a: e.tensor_tensor(out=av[:, :, :, a, :], in0=xv[:, :, :, a, :], in1=cosb.unsqueeze(1).to_broadcast([128, nh, 2, 32]), op=ALU.mult), [xq(), rcos_s()], [A_()])
            K.op("dve", lambda e, a=a: e.tensor_tensor(out=bv[:, :, :, a, :], in0=xv[:, :, :, 1 - a, :], in1=sinb.unsqueeze(1).to_broadcast([128, nh, 2, 32]), op=ALU.mult), [xq(), rsin_s()], [B_()])
        K.op("dve", lambda e: e.tensor_tensor(out=ov[:, :, :, 0, :], in0=av[:, :, :, 0, :], in1=bv[:, :, :, 0, :], op=ALU.subtract), [A_(), B_()], [out_bf()])
        K.op("dve", lambda e: e.tensor_tensor(out=ov[:, :, :, 1, :], in0=av[:, :, :, 1, :], in1=bv[:, :, :, 1, :], op=ALU.add), [A_(), B_()], [out_bf()])

    def qk_norm(P, src, nh, gain, outf):
        ps_ap = src.ap
        sq = P["ropeA"].next()
        K.op("act", lambda e: e.activation(out=sq.t[:, 0:nh * 128], in_=ps_ap, func=AF.Square), [src], [sq()])
        s8 = P["s8"].next()
        K.op("dve", lambda e: e.tensor_reduce(out=s8.t[:, 0:nh], in_=sq.t[:, 0:nh * 128].rearrange("p (h c) -> p h c", h=nh), axis=AX.X, op=ALU.add), [sq()], [s8()])
        K.op("act", lambda e: e.activation(out=s8.t[:, 0:nh], in_=s8.t[:, 0:nh], func=AF.Sqrt, scale=1.0 / 128, bias=eps_s.t[:, 0:1]), [s8(), eps_s()], [s8()])
        K.op("dve", lambda e: e.reciprocal(out=s8.t[:, 8:8 + nh], in_=s8.t[:, 0:nh]), [s8()], [s8()])
        ov = outf.t[:, 0:nh * 128].rearrange("p (h c) -> p h c", h=nh)
        K.op("dve", lambda e: e.tensor_tensor(out=ov, in0=ps_ap.rearrange("p (h c) -> p h c", h=nh), in1=s8.t[:, 8:8 + nh].unsqueeze(2).to_broadcast([128, nh, 128]), op=ALU.mult), [s8(), src], [outf()])
        K.op("dve", lambda e: e.tensor_tensor(out=ov, in0=ov, in1=gain.t[:].unsqueeze(1).to_broadcast([128, nh, 128]), op=ALU.mult), [outf(), gain()], [outf()])

    K.push_scope()
    PD = {"ropeA": K.pool("ropeA", [128, 1024], F32, 2), "ropeB": K.pool("ropeB", [128, 1024], F32, 2), "s8": K.pool("s8", [128, 16], F32, 3)}
    wkv = K.sb("wkv", [128, 8, 512], BF16)
    K.dma("sp", wkv(), WB["w_c_kv"]())
    knp = K.pool("knp", [128, 256], F32, 2)
    kbp = K.pool("kbp", [128, 256], BF16, 2)
    vfp = K.pool("vfp", [128, 256], F32, 2)
    vbp = K.pool("vbp", [128, 256], BF16, 2)
    ktp2 = K.pool("ktp2", [128, 2, 128], BF16, 2)
    SkT_view = S_kv.t[0:2048, :].rearrange("(p a) c -> p (a c)", p=128).rearrange("p (k t) -> p k t", k=2)
    for g in range(5):
        tok0 = g * 512
        hT = norm_mod_T([(S_x1, tok0 + i * 128) for i in range(4)], 1, 0, 0 if g < 4 else 1)
        for i in range(4):
            t = g * 4 + i
            ps = PF.next()
            for kt in range(8):
                K.op("pe", lambda e, kt=kt, ps=ps, i=i: e.matmul(ps.t[:], lhsT=hT.t[:, kt, i * 128:(i + 1) * 128], rhs=wkv.t[:, kt, :], start=(kt == 0), stop=(kt == 7)), [hT(key=kt), wkv()], [ps()])
            kn = knp.next()
            qk_norm(PD, ps(np.s_[:, 0:256]), 2, kg_s, kn)
            kb = kbp.next()
            vb = vbp.next()
            if g < 4:
                rope(PD, kn, 2, t, kb)
                K.op("act", lambda e, ps=ps, vb=vb: e.activation(out=vb.t[:], in_=ps.t[:, 256:512], func=AF.Copy), [ps()], [vb()])
                K.dma("pool", S_kv(np.s_[2048 + t * 128:2048 + (t + 1) * 128, :], key=("v", t)), vb())
            else:
                c0 = (t - 16) * 128
                K.dma("pool", ok_(np.s_[c0:c0 + 128, :]), kn())
                vf = vfp.next()
                K.op("act", lambda e, ps=ps, vf=vf: e.activation(out=vf.t[:], in_=ps.t[:, 256:512], func=AF.Copy), [ps()], [vf()])
                K.dma("pool", ov_(np.s_[c0:c0 + 128, :]), vf())
                K.op("dve", lambda e, kn=kn, kb=kb: e.tensor_copy(out=kb.t[:], in_=kn.t[:]), [kn()], [kb()])
                K.op("dve", lambda e, vf=vf, vb=vb: e.tensor_copy(out=vb.t[:], in_=vf.t[:]), [vf()], [vb()])
                K.dma("pool", S_cv.v(S_cv.t[c0:c0 + 128].rearrange("t k d -> t (k d)"), key=t), vb())
            pb = PB.next()
            for kv in range(2):
                K.op("pe", lambda e, kv=kv, kb=kb, pb=pb: e.transpose(out=pb.t[:, kv * 128:(kv + 1) * 128], in_=kb.t[:, kv * 128:(kv + 1) * 128], identity=identb.t[:]), [kb(), identb()], [pb()])
            kT2 = ktp2.next()
            K.op("act", lambda e, pb=pb, kT2=kT2: e.activation(out=kT2.t[:].rearrange("p k t -> p (k t)"), in_=pb.t[:, 0:256], func=AF.Copy), [pb()], [kT2()])
            if g < 4:
                K.dma("pool", S_kv.v(SkT_view[:, :, t * 128:(t + 1) * 128], key=("k", t)), kT2())
            else:
                c0 = (t - 16) * 128
                K.dma("pool", S_ckT(np.s_[:, :, c0:c0 + 128], key=t), kT2())
    K.pop_scope()
    K.cc(lambda e: e.collective_compute("AllGather", ALU.bypass, replica_groups=[[0, 1], [2, 3], [4, 5], [6, 7]], ins=[S_kv.t[:, :]], outs=[S_kvg.t[:, :]]),
         [S_kv()], [S_kvg()])

    if stage <= 9:
        return K, es, locals()
    K.push_scope()
    P = post_scope_alloc(need_aT=False)
    P.update({"ropeA": K.pool("ropeA2", [128, 1024], F32, 1), "ropeB": K.pool("ropeB2", [128, 1024], F32, 1), "s8": K.pool("s8b", [128, 16], F32, 3)})
    load_wo(P, "w_oc")
    wqp = K.pool("wq", [128, 8, 512], BF16, 1)
    KT_all = K.sb("KT_all", [128, 2, NKEY], BF16)
    V_all = K.sb("V_all", [128, NKC, 2, 129], BF16)
    K.op("dve", lambda e: e.memset(V_all.t[:, :, :, 128:129], 1.0), [], [V_all(key="one")])
    for r in range(2):
        kview = S_kvg.t[r * 4096:r * 4096 + 2048, :].rearrange("(p a) c -> p (a c)", p=128).rearrange("p (k t) -> p k t", k=2)
        K.dma("sp", KT_all(np.s_[:, :, r * 2048:(r + 1) * 2048], key=("k", r)), S_kvg.v(kview))
        vview = S_kvg.t[r * 4096 + 2048:(r + 1) * 4096, :].rearrange("(c p) (k d) -> p c k d", p=128, k=2)
        for kv in range(2):
            K.dma("sp", V_all(np.s_[:, r * 16:(r + 1) * 16, kv, 0:128], key=("v", r, kv)), S_kvg.v(vview[:, :, kv, :]))
    K.dma("pool", KT_all(np.s_[:, :, 4096:NKEY], key=("k", 2)), ckT())
    for kv in range(2):
        K.dma("pool", V_all(np.s_[:, 32:34, kv, 0:128], key=("v", 2, kv)), cv(np.s_[:, :, kv, :]))
    KTc = K.pool("KTc", [128, 2, 256], BF16, 1)
    Vc = K.pool("Vc", [128, 2, 2, 129], BF16, 1)
    qfp = K.pool("qfp", [128, 1024], F32, 1)
    qbp = K.pool("qbp", [128, 1024], BF16, 1)
    QTp = K.pool("QTp", [128, 8, 128], BF16, 1)
    Pp = K.pool("Pp", [128, 512], BF16, 2)
    Op_ = K.pool("Ob", [128, 8, 128], BF16, 1)
    OTp = K.pool("OTp", [128, 8, 128], BF16, 1)
    r8p = K.pool("r8p", [128, 8], F32, 2)
    SCALE = 128 ** -0.5
    for g in range(5):
        tok0 = g * 512
        j = 0 if g < 4 else 1
        hT = norm_mod_T([(S_x1, tok0 + i * 128) for i in range(4)], 1, 0, j)
        x1s = []
        for i in range(4):
            t = g * 4 + i
            qf = qfp.next()
            for half in range(2):
                ps = PF.next()
                wqb = wqp.next()
                K.dma("sp", wqb(), WB["w_c_q"].v(WB["w_c_q"].t[half]))
                for kt in range(8):
                    K.op("pe", lambda e, kt=kt, ps=ps, i=i, half=half, wqb=wqb: e.matmul(ps.t[:], lhsT=hT.t[:, kt, i * 128:(i + 1) * 128], rhs=wqb.t[:, kt, :], start=(kt == 0), stop=(kt == 7)), [hT(key=kt), wqb()], [ps()])
                K.op("act", lambda e, ps=ps, qf=qf, half=half: e.activation(out=qf.t[:, half * 512:(half + 1) * 512], in_=ps.t[:], func=AF.Copy), [ps()], [qf(key=half)])
            qn = qf
            qk_norm(P, qf(), 8, qg_s, qn)
            qb = qbp.next()
            if g < 4:
                rope(P, qn, 8, t, qb)
                KT_use, V_use, nkc = KT_all, V_all, NKC
                kt_ap = lambda kv, kc: KT_all.t[:, kv, kc * 128:(kc + 1) * 128]
                v_ap = lambda kv, kc: V_all.t[:, kc, kv, :]
            else:
                K.op("dve", lambda e, qn=qn, qb=qb: e.tensor_copy(out=qb.t[:], in_=qn.t[:]), [qn()], [qb()])
                s = (t - 16) // 2
                if (t - 16) % 2 == 0:
                    KTcb = KTc.next()
                    Vcb = Vc.next()
                    K.dma("sp", KTcb(), S_ckT(np.s_[:, :, s * 256:(s + 1) * 256]))
                    K.op("dve", lambda e, Vcb=Vcb: e.memset(Vcb.t[:, :, :, 128:129], 1.0), [], [Vcb()])
                    for kv_ in range(2):
                        K.dma("sp", Vcb(np.s_[:, :, kv_, 0:128]), S_cv.v(S_cv.t[s * 256:(s + 1) * 256].rearrange("(c p) k d -> p c k d", p=128)[:, :, kv_, :]))
                KT_use, V_use, nkc = KTcb, Vcb, 2
                kt_ap = lambda kv, kc, KTcb=KTcb: KTcb.t[:, kv, kc * 128:(kc + 1) * 128]
                v_ap = lambda kv, kc, Vcb=Vcb: Vcb.t[:, kc, kv, :]
            pb = PB.next()
            for h in range(8):
                K.op("pe", lambda e, h=h, qb=qb, pb=pb: e.transpose(out=pb.t[:, h * 128:(h + 1) * 128], in_=qb.t[:, h * 128:(h + 1) * 128], identity=identb.t[:]), [qb(), identb()], [pb()])
            QT = QTp.next()
            K.op("act", lambda e, pb=pb, QT=QT: e.activation(out=QT.t[:].rearrange("p h c -> p (h c)"), in_=pb.t[:], func=AF.Copy), [pb()], [QT()])
            Ob = Op_.next()
            for kv in range(2):
                acc = [PH.next(), PH.next()]
                accv = [b_.t[:, 0:258].rearrange("p (h c) -> p h c", h=2) for b_ in acc]
                for kc in range(nkc):
                    psS = PF.next()
                    K.op("pe", lambda e, kv=kv, kc=kc, psS=psS, QT=QT, kt_ap=kt_ap: e.matmul(psS.t[:], lhsT=kt_ap(kv, kc), rhs=QT.t[:, kv * 4:(kv + 1) * 4, :].rearrange("p h c -> p (h c)"), start=True, stop=True), [KT_use(), QT()], [psS()])
                    Pb = Pp.next()
                    K.op("act", lambda e, psS=psS, Pb=Pb: e.activation(out=Pb.t[:], in_=psS.t[:], func=AF.Exp, scale=SCALE), [psS()], [Pb()])
                    for hh in range(4):
                        K.op("pe", lambda e, hh=hh, kv=kv, kc=kc, Pb=Pb, accv=accv, v_ap=v_ap, nkc=nkc: e.matmul(accv[hh // 2][:, hh % 2, :], lhsT=Pb.t[:, hh * 128:(hh + 1) * 128], rhs=v_ap(kv, kc), start=(kc == 0 and hh % 2 == 0), stop=(kc == nkc - 1 and hh % 2 == 1)), [Pb(), V_use()], [acc[hh // 2]()])
                r8 = r8p.next()
                for b2 in range(2):
                    K.op("dve", lambda e, b2=b2, r8=r8, accv=accv: e.reciprocal(out=r8.t[:, 2 * b2:2 * b2 + 2], in_=accv[b2][:, :, 128]), [acc[b2]()], [r8()])
                    K.op("dve", lambda e, b2=b2, r8=r8, accv=accv, kv=kv, Ob=Ob: e.tensor_tensor(out=Ob.t[:, kv * 4 + 2 * b2:kv * 4 + 2 * b2 + 2, :], in0=accv[b2][:, :, 0:128], in1=r8.t[:, 2 * b2:2 * b2 + 2].unsqueeze(2).to_broadcast([128, 2, 128]), op=ALU.mult), [acc[b2](), r8()], [Ob()])
            pb2 = PB.next()
            for h in range(8):
                K.op("pe", lambda e, h=h, Ob=Ob, pb2=pb2: e.transpose(out=pb2.t[:, h * 128:(h + 1) * 128], in_=Ob.t[:, h, :], identity=identb.t[:]), [Ob(), identb()], [pb2()])
            OT = OTp.next()
            K.op("act", lambda e, pb2=pb2, OT=OT: e.activation(out=OT.t[:].rearrange("p h c -> p (h c)"), in_=pb2.t[:], func=AF.Copy), [pb2()], [OT()])
            xt_ = load_x(S_x1, tok0 + i * 128)
            x1s.append(outproj_residual(P, 1, j, OT, 0, xt_()))
        ffn_group(P, 1, j, x1s, tok0, final=True)
    K.pop_scope()
    return K, es, locals()

def _blk_f(W, c0, nblk):
    Wc = W[:, c0:c0 + nblk * 128].reshape(8, 128, nblk, 128)
    return np.ascontiguousarray(Wc.transpose(2, 1, 0, 3))


def _blk_t(W, c0, width):
    return np.ascontiguousarray(W[:, c0:c0 + width].reshape(8, 128, width).transpose(1, 0, 2))


def _fp(v, n):
    return np.ascontiguousarray(v.reshape(n, 128).T)


_NC_CACHE = {}


def prepare(inputs):
    I = {k_: np.asarray(v) for k_, v in inputs.items()}
    f32 = np.float32
    xp, xsam, c = I["x_prompt"], I["x_sample"], I["c"]
    w_mod, b_mod = I["w_mod"], I["b_mod"]
    fidx = (0, 1, 3, 4)
    wmodA = np.stack([np.concatenate([_blk_f(w_mod[l], fi * 1024, 8) for fi in fidx], 0) for l in range(2)], 0)
    bmodA = np.stack([np.concatenate([_fp(b_mod[l, fi * 1024:(fi + 1) * 1024], 8) for fi in fidx], 1) for l in range(2)], 1)
    gcols = (2048, 2560, 5120, 5632)
    wmodG = np.stack([np.stack([_blk_t(w_mod[l], cc_, 512) for cc_ in gcols], 0) for l in range(2)], 0)
    bmodG = np.stack([np.broadcast_to(np.concatenate([b_mod[l, 2048:3072], b_mod[l, 5120:6144]])[None, :], (128, 2048)) for l in range(2)], 0)
    n1g = np.stack([_fp(I["norm1_g"][l], 8) for l in range(2)], 1)
    n2g = np.stack([_fp(I["norm2_g"][l], 8) for l in range(2)], 1)
    Wab = I["w_in_ab"][0]
    bgate = I["b_gate_ab"][0]
    w_ab_t = np.stack([_blk_t(Wab, 1536 + 512 * i_, 512) for i_ in range(3)], 0)
    conv_w = I["conv_w"][0]
    cb = _fp(I["conv_b"][0], 4)
    lng = _fp(I["conv_ln_g"][0], 4)
    lnb = _fp(I["conv_ln_b"][0], 4)
    mng = np.broadcast_to(I["mlstm_norm_g"][0][None, :], (128, 512))
    Woab = I["w_out_ab"][0]
    w_oab = np.stack([_blk_t(Woab, h_ * 512, 512) for h_ in range(2)], 0)
    Wc = I["w_in_c"][0]
    w_c_q = np.stack([_blk_t(Wc, h_ * 512, 512) for h_ in range(2)], 0)
    w_c_kv = _blk_t(Wc, 1024, 512)
    qg = np.broadcast_to(I["q_norm_g"][0][None, :], (128, 128))
    kg = np.broadcast_to(I["k_norm_g"][0][None, :], (128, 128))
    Woc = I["w_out_c"][0]
    w_oc = np.stack([_blk_t(Woc, h_ * 512, 512) for h_ in range(2)], 0)
    wfi = I["w_ffn_in"]
    w_fi = np.stack([np.concatenate([_blk_f(wfi[l], 0, NFB), _blk_f(wfi[l], DFF, NFB)], axis=3) for l in range(2)], 0)
    wfo = I["w_ffn_out"]
    w_fo = np.stack([np.stack([np.ascontiguousarray(wfo[l][:, h_ * 512:(h_ + 1) * 512].reshape(NFB, 128, 512).transpose(1, 0, 2)) for h_ in range(2)], 0) for l in range(2)], 0)
    fng = np.broadcast_to(I["final_norm_g"][None, :], (128, D))
    ident = np.eye(128, dtype=f32)
    s_ = np.arange(128)
    same = (s_[:, None] // 64) == (s_[None, :] // 64)
    maskA = (same & (s_[:, None] <= s_[None, :])).astype(f32)
    maskB = (same & (s_[:, None] >= s_[None, :])).astype(f32)
    inv = (1.0 / (np.float32(10000.0) ** (np.arange(32, dtype=f32) / np.float32(32)))).astype(f32)

    shared = dict(wmodA=wmodA, bmodA=bmodA, wmodG=wmodG, bmodG=bmodG, n1g=n1g, n2g=n2g, w_ab_t=w_ab_t, cb=cb, lng=lng, lnb=lnb,
                  mng=mng, w_oab=w_oab, w_c_q=w_c_q, w_c_kv=w_c_kv, qg=qg, kg=kg, w_oc=w_oc, w_fi=w_fi, w_fo=w_fo, fng=fng,
                  ident=ident, maskA=maskA, maskB=maskB)
    shared = {k_: np.ascontiguousarray(v, dtype=f32) for k_, v in shared.items()}
    base_f = _blk_f(Wab, 0, 16)

    def gate_blocks(rev):
        gi = [(0, 4), (8, 12)] if not rev else [(8, 12), (0, 4)]
        blks = np.zeros((2, 128, 8, 128), f32)
        bgv = np.zeros((128, 2), f32)
        for bi, (i0, _) in enumerate(gi):
            Wi = Wab[:, 3072 + i0:3072 + i0 + 4].reshape(8, 128, 4).transpose(1, 0, 2)
            Wf = Wab[:, 3072 + i0 + 4:3072 + i0 + 8].reshape(8, 128, 4).transpose(1, 0, 2)
            blks[bi, :, :, 0:4] = Wi
            blks[bi, :, :, 32:36] = Wf
            bgv[0:4, bi] = bgate[i0:i0 + 4]
            bgv[32:36, bi] = bgate[i0 + 4:i0 + 8]
        return blks, bgv

    in_maps = []
    for i in range(8):
        b, half = i // 2, i % 2
        rev = half == 1
        if not rev:
            xl = xsam[b, 0:2048]
            halo = xsam[b, 2048:2063]
            pos = np.arange(2048)
        else:
            xl = xsam[b, 2048:4096][::-1]
            halo = xsam[b, 2047:2032:-1]
            pos = 4095 - np.arange(2048)
        xc = [xp[2 * i + s][::-1] if rev else xp[2 * i + s] for s in range(2)]
        hal = np.zeros((128, D), f32)
        hal[0:15] = halo
        xs = np.concatenate([xl, xc[0], xc[1], hal], 0)
        conds = np.stack([c[b], I["c_ctx"]], 0)
        cT = np.ascontiguousarray(conds.reshape(2, 8, 128).transpose(2, 1, 0))
        d_ = 1 if rev else 0
        C0 = I["state_mlstm_C"][b, 0, d_]
        n0 = I["state_mlstm_n"][b, 0, d_]
        st_C = np.concatenate([C0.transpose(1, 0, 2), n0.T[:, :, None]], 2)
        st_m = I["state_mlstm_m"][b, 0, d_].reshape(4, 1)
        sel = np.zeros((128, 2), f32)
        sel[:, 1 - half] = 1.0
        ckT = I["cache_k"][b, 0].transpose(2, 1, 0)
        cv = I["cache_v"][b, 0].reshape(2, 128, 2, 128).transpose(1, 0, 2, 3)
        row = (pos // 64).astype(f32)
        col = (pos % 64).astype(f32)
        ang = np.concatenate([row[:, None] * inv[None, :], col[:, None] * inv[None, :]], 1).astype(f32)
        rcos = np.cos(ang).astype(f32).reshape(NLT, 128, 64).transpose(1, 0, 2)
        rsin = np.sin(ang).astype(f32).reshape(NLT, 128, 64).transpose(1, 0, 2)
        gb, bgv = gate_blocks(rev)
        w_ab_f = np.concatenate([base_f, gb], 0)
        cwj = conv_w[::-1] if rev else conv_w
        cw = cwj.reshape(31, 4, 128).transpose(2, 1, 0)
        m = dict(shared)
        m.update(xs=xs, cT=cT, st_C=st_C, st_m=st_m, sel=sel, ckT=ckT, cv=cv, rcos=rcos, rsin=rsin, w_ab_f=w_ab_f, bg=bgv, cw=cw)
        in_maps.append({k_: np.ascontiguousarray(v, dtype=f32) for k_, v in m.items()})

    return in_maps


def get_nc(stage=99, debug=False):
    key = (stage, debug)
    if key not in _NC_CACHE:
        nc = bass.Bass("TRN2", target_bir_lowering=False)
        K, es, _ = build(nc, stage=stage, debug=debug)
        K.barrier_final()
        K.emit()
        es.close()
        _NC_CACHE[key] = nc
    return _NC_CACHE[key]


def kernel(**inputs):
    in_maps = prepare(inputs)
    f32 = np.float32
    if False:
        nc = bass.Bass("TRN2", target_bir_lowering=False)
        K, es, _ = build(nc)
        K.barrier_final()
        K.emit()
        es.close()
        _NC_CACHE["nc"] = nc
    nc = get_nc()
    res = run_bass_kernel_spmd(nc, in_maps, core_ids=list(range(8)))
    R_ = res.results

    y_prompt = np.zeros((16, 256, D), f32)
    y_sample = np.zeros((4, 4096, D), f32)
    nC = np.zeros((16, 1, 2, 4, 128, 128), f32)
    nn = np.zeros((16, 1, 2, 4, 128), f32)
    nm = np.zeros((16, 1, 2, 4), f32)
    nk = np.zeros((16, 1, 256, 2, 128), f32)
    nv = np.zeros((16, 1, 256, 2, 128), f32)
    for i in range(8):
        b, half = i // 2, i % 2
        rev = half == 1
        r = R_[i]
        yy = r["y"]
        if not rev:
            y_sample[b, 0:2048] = yy[0:2048]
        else:
            y_sample[b, 2048:4096] = yy[0:2048][::-1]
        for s in range(2):
            blk = yy[2048 + s * 256:2048 + (s + 1) * 256]
            kk = r["ok"][s * 256:(s + 1) * 256].reshape(256, 2, 128)
            vv = r["ov"][s * 256:(s + 1) * 256].reshape(256, 2, 128)
            if rev:
                blk, kk, vv = blk[::-1], kk[::-1], vv[::-1]
            y_prompt[2 * i + s] = blk
            nk[2 * i + s, 0] = kk
            nv[2 * i + s, 0] = vv
            for pas in range(2):
                d_ = pas if not rev else 1 - pas
                nC[2 * i + s, 0, d_] = r["oC"][s, pas]
                nn[2 * i + s, 0, d_] = r["on"][s, pas]
                nm[2 * i + s, 0, d_] = r["om"][s, pas]
    return (y_prompt, y_sample, nC, nn, nm, nk, nv)
```

```python
import numpy as np
from concourse.bass_utils import run_bass_kernel_spmd

class R:
    __slots__ = ("buf", "ap", "key")

    def __init__(self, buf, ap, key=None):
        self.buf = buf
        self.ap = ap
        self.key = key


class Buf:
    def __init__(self, t, name, is_ap=False):
        self.t = t
        self.name = name
        self.w = {}
        self.r = {}
        self.is_ap = is_ap

    def __call__(self, idx=None, key=None):
        if idx is None:
            ap = self.t if self.is_ap else self.t[:]
        else:
            ap = self.t[idx]
        return R(self, ap, key)

    def v(self, ap, key=None):
        return R(self, ap, key)


class _Rec:
    def __init__(self):
        self.call = None

    def __getattr__(self, name):
        def f(*a, **kw):
            assert self.call is None
            self.call = (name, a, kw)
            return None
        return f


def _eager(fn):
    r = _Rec()
    fn(r)
    name, a, kw = r.call
    return lambda e: getattr(e, name)(*a, **kw)


class Op:
    __slots__ = ("eng", "fn", "deps", "sig", "val", "kind", "sem", "n")

    def __init__(self, eng, fn, kind):
        self.eng = eng
        self.fn = fn
        self.deps = []
        self.sig = False
        self.val = 0
        self.kind = kind
        self.sem = None
        self.n = 0


def _conf(k1, k2):
    return k1 is None or k2 is None or k1 == k2


class Pool:
    def __init__(self, bufs):
        self.bufs = bufs
        self.i = 0

    def next(self):
        b = self.bufs[self.i % len(self.bufs)]
        self.i += 1
        return b


class Sched:
    ENGS = ("pe", "act", "dve", "pool", "sp")

    def __init__(self, nc, es, ndma=None):
        self.nc = nc
        self.es = es
        self.ops = {e: [] for e in self.ENGS}
        self.nops = 0
        self.csem = {}
        for e in ("pe", "act", "dve", "pool"):
            self.csem[e] = es.enter_context(nc.semaphore("c_" + e))
        ndma = ndma or {"sp": 20, "pool": 12, "act": 6}
        self.dsem = {}
        self.dcnt = {}
        self.dhist = {}
        for q, n in ndma.items():
            self.dsem[q] = [es.enter_context(nc.semaphore("d_%s%d" % (q, i))) for i in range(n)]
            self.dcnt[q] = 0
            self.dhist[q] = []
        self.ccsem = es.enter_context(nc.semaphore("ccs"))
        self.cccnt = 0
        self.all_dma = []
        self.scopes = []

    def sb(self, name, shape, dt):
        st = self.scopes[-1] if self.scopes else self.es
        self.uid = getattr(self, "uid", 0) + 1
        name = "%s_u%d" % (name, self.uid)
        return Buf(st.enter_context(self.nc.sbuf_tensor(name, list(shape), dt)), name)

    def scope(self):
        import contextlib
        K = self

        @contextlib.contextmanager
        def cm():
            st = contextlib.ExitStack()
            K.scopes.append(st)
            try:
                yield
            finally:
                K.scopes.pop()
                K.fence()
                st.close()
        return cm()

    def push_scope(self):
        import contextlib
        self.scopes.append(contextlib.ExitStack())

    def pop_scope(self):
        st = self.scopes.pop()
        self.fence()
        st.close()

    def fence(self):
        lastc = []
        for e in ("pe", "act", "dve", "pool"):
            for x in reversed(self.ops[e]):
                if x.kind == "c":
                    lastc.append(x)
                    break
        lastd = []
        for q in self.dhist:
            ns = len(self.dsem[q])
            lastd.extend(self.dhist[q][-ns:])
        lastcc = None
        for x in reversed(self.ops["pool"]):
            if x.kind == "cc":
                lastcc = x
                break
        for e in self.ENGS:
            o = Op(e, None, "nop")
            for x in lastc:
                x.sig = True
                o.deps.append(x)
            o.deps.extend(lastd)
            if lastcc is not None:
                o.deps.append(lastcc)
            self.ops[e].append(o)

    def ps(self, name, shape, dt):
        return Buf(self.es.enter_context(self.nc.psum_tensor(name, list(shape), dt)), name)

    def dram(self, name, shape, dt, kind="Internal"):
        return Buf(self.nc.dram_tensor(name, list(shape), dt, kind=kind).ap(), name, is_ap=True)

    def pool(self, name, shape, dt, n, space="sb"):
        mk = self.sb if space == "sb" else self.ps
        return Pool([mk("%s%d" % (name, i), shape, dt) for i in range(n)])

    def _track(self, op, reads, writes):
        deps = op.deps
        for r in reads:
            if r is None or r.buf is None:
                continue
            b = r.buf
            for k2, w in b.w.items():
                if _conf(r.key, k2):
                    deps.append(w)
        for w_ in writes:
            b = w_.buf
            for k2, w in b.w.items():
                if _conf(w_.key, k2):
                    deps.append(w)
            for k2, rl in b.r.items():
                if _conf(w_.key, k2):
                    deps.extend(rl)
        for r in reads:
            if r is None or r.buf is None:
                continue
            r.buf.r.setdefault(r.key, []).append(op)
        for w_ in writes:
            b = w_.buf
            if w_.key is None:
                b.w = {None: op}
                b.r = {}
            else:
                b.w[w_.key] = op
                b.r[w_.key] = []
        out = []
        seen = set()
        for d in deps:
            if d is op or id(d) in seen:
                continue
            seen.add(id(d))
            if d.eng == "pe" and op.eng == "pe" and d.kind == "c" and op.kind == "c":
                continue
            out.append(d)
            d.sig = True
        op.deps = out

    def op(self, eng, fn, reads=(), writes=()):
        o = Op(eng, _eager(fn), "c")
        o.sem = self.csem[eng]
        o.n = self.nops
        self.nops += 1
        self._track(o, reads, writes)
        self.ops[eng].append(o)
        return o

    def dma(self, q, out, in_, extra_reads=(), **kw):
        o = Op(q, (lambda e, o_=out.ap, i_=in_.ap, kw_=kw: e.dma_start(out=o_, in_=i_, **kw_)), "dma")
        o.n = self.nops
        self.nops += 1
        n = self.dcnt[q]
        ns = len(self.dsem[q])
        o.sem = self.dsem[q][n % ns]
        o.val = 16 * (n // ns + 1)
        self.dcnt[q] += 1
        self._track(o, [in_] + list(extra_reads), [out])
        if n >= ns:
            prev = self.dhist[q][n - ns]
            if prev not in o.deps:
                o.deps.append(prev)
        self.dhist[q].append(o)
        self.ops[q].append(o)
        self.all_dma.append(o)
        return o

    def cc(self, fn, reads, writes):
        o = Op("pool", _eager(fn), "cc")
        o.n = self.nops
        self.nops += 1
        o.sem = self.ccsem
        self.cccnt += 1
        o.val = self.cccnt
        self._track(o, reads, writes)
        self.ops["pool"].append(o)
        return o

    def barrier_final(self):
        o = Op("sp", None, "nop")
        for q in self.dhist:
            ns = len(self.dsem[q])
            for d in self.dhist[q][-ns:]:
                o.deps.append(d)
        for e in ("pe", "act", "dve", "pool"):
            for x in reversed(self.ops[e]):
                if x.kind == "c":
                    x.sig = True
                    o.deps.append(x)
                    break
        if self.cccnt:
            for x in reversed(self.ops["pool"]):
                if x.kind == "cc":
                    o.deps.append(x)
                    break
        self.ops["sp"].append(o)

    def emit(self):
        nc = self.nc
        for e in ("pe", "act", "dve", "pool"):
            c = 0
            for o in self.ops[e]:
                if o.kind == "c" and o.sig:
                    c += 1
                    o.val = c
        self.stats = {e: len(self.ops[e]) for e in self.ENGS}
        nwaits = {e: 0 for e in self.ENGS}

        def run(engname):
            def f(eng):
                seen = {}
                for o in self.ops[engname]:
                    for d in o.deps:
                        k = id(d.sem)
                        if seen.get(k, 0) < d.val:
                            eng.wait_ge(d.sem, d.val)
                            seen[k] = d.val
                            nwaits[engname] += 1
                    if o.kind == "nop":
                        continue
                    ins = o.fn(eng)
                    if o.kind == "dma":
                        ins.then_inc(o.sem, 16)
                    elif o.kind == "cc":
                        ins.then_inc(o.sem, 1)
                    elif o.sig:
                        ins.then_inc(o.sem, 1)
            return f

        with nc.Block() as block:
            block.tensor(run("pe"))
            block.scalar(run("act"))
            block.vector(run("dve"))
            block.gpsimd(run("pool"))
            block.sync(run("sp"))
        self.stats["waits"] = nwaits

import numpy as np
from contextlib import ExitStack
import concourse.bass as bass
import concourse.mybir as mybir

F32 = mybir.dt.float32
BF16 = mybir.dt.bfloat16
AF = mybir.ActivationFunctionType
ALU = mybir.AluOpType
AX = mybir.AxisListType

D = 1024
NLT = 16
NCT = 4
NTT = 20
NT = 21
NTOK = NTT * 128
DFF = 2816
NFB = 22
EPS = 1e-6
NKEY = 4096 + 256
NKC = NKEY // 128

STAGE = 99


def build(nc, stage=99, debug=False):
    es = ExitStack()
    K = Sched(nc, es)
    K.debug = debug
    IN = {}

    def inp(name, shape, dt=F32):
        b = Buf(nc.dram_tensor(name, list(shape), dt, kind="ExternalInput").ap(), name, is_ap=True)
        IN[name] = b
        return b

    def outp(name, shape):
        return Buf(nc.dram_tensor(name, list(shape), F32, kind="ExternalOutput").ap(), name, is_ap=True)

    xs = inp("xs", [NT * 128, D])
    cT = inp("cT", [128, 8, 2])
    st_C = inp("st_C", [128, 4, 129])
    st_m = inp("st_m", [4, 1])
    sel = inp("sel", [128, 2])
    ckT = inp("ckT", [128, 2, 256])
    cv = inp("cv", [128, 2, 2, 128])
    rcos = inp("rcos", [128, NLT, 64])
    rsin = inp("rsin", [128, NLT, 64])
    ident_in = inp("ident", [128, 128])
    maskA_in = inp("maskA", [128, 128])
    maskB_in = inp("maskB", [128, 128])
    wmodA = inp("wmodA", [2, 32, 128, 8, 128])
    bmodA = inp("bmodA", [128, 2, 32])
    wmodG = inp("wmodG", [2, 4, 128, 8, 512])
    bmodG = inp("bmodG", [2, 128, 2048])
    n1g = inp("n1g", [128, 2, 8])
    n2g = inp("n2g", [128, 2, 8])
    w_ab_f = inp("w_ab_f", [18, 128, 8, 128])
    w_ab_t = inp("w_ab_t", [3, 128, 8, 512])
    bg = inp("bg", [128, 2])
    cw = inp("cw", [128, 4, 31])
    cb = inp("cb", [128, 4])
    lng = inp("lng", [128, 4])
    lnb = inp("lnb", [128, 4])
    mng = inp("mng", [128, 512])
    w_oab = inp("w_oab", [2, 128, 8, 512])
    w_c_q = inp("w_c_q", [2, 128, 8, 512])
    w_c_kv = inp("w_c_kv", [128, 8, 512])
    qg = inp("qg", [128, 128])
    kg = inp("kg", [128, 128])
    w_oc = inp("w_oc", [2, 128, 8, 512])
    w_fi = inp("w_fi", [2, NFB, 128, 8, 256])
    w_fo = inp("w_fo", [2, 2, 128, NFB, 512])
    fng = inp("fng", [128, D])

    y = outp("y", [NTOK, D])
    oC = outp("oC", [2, 2, 4, 128, 128])
    on = outp("on", [2, 2, 4, 128])
    om = outp("om", [2, 2, 4])
    ok_ = outp("ok", [512, 256])
    ov_ = outp("ov", [512, 256])

    _kd = K.dram

    def _dram_dbg(name, shape, dt, kind="Internal"):
        if debug and name.startswith("S_"):
            kind = "ExternalOutput"
        return _kd(name, shape, dt, kind=kind)
    K.dram = _dram_dbg
    DBG = {}

    def dump(name, buf):
        if not debug:
            return
        shp = list(buf.t.shape)
        o = Buf(nc.dram_tensor("D_" + name, shp, buf.t.dtype, kind="ExternalOutput").ap(), "D_" + name, is_ap=True)
        K.dma("sp", o(), buf())

    def wscr(src):
        return K.dram(src.name + "_b", list(src.t.shape), BF16)

    WB = {}
    for b_ in (w_ab_f, w_ab_t, w_oab, w_c_q, w_c_kv, w_oc, w_fi, w_fo):
        WB[b_.name] = wscr(b_)

    def cast_weight(src, pieces):
        dst = WB[src.name]
        shp = list(src.t.shape)
        n0 = shp[0]
        assert n0 % pieces == 0
        st = n0 // pieces
        for i in range(pieces):
            K.dma("pool", dst(np.s_[i * st:(i + 1) * st], key=("w", i)), src(np.s_[i * st:(i + 1) * st]))

    S_qT = K.dram("S_qT", [128, 4, NTOK], BF16)
    S_kT = K.dram("S_kT", [128, 4, NTOK], BF16)
    S_k = K.dram("S_k", [NTOK, 512], BF16)
    S_v = K.dram("S_v", [NTOK, 4, 129], BF16)
    S_o = K.dram("S_o", [NTOK, 512], BF16)
    S_hf = K.dram("S_hf", [NTOK, 512], F32)
    S_act = K.dram("S_act", [128, 8, NTOK], BF16)
    S_x1 = K.dram("S_x1", [NTOK, D], F32)
    S_kv = K.dram("S_kv", [4096, 256], BF16)
    S_kvg = K.dram("S_kvg", [8192, 256], BF16)
    S_st = K.dram("S_st", [128, 520], F32)
    S_stg = K.dram("S_stg", [256, 520], F32)

    def cload(src, shape, dt=F32, name=None):
        b = K.sb(name or ("c_" + src.name), shape, dt)
        K.dma("sp", b(), src())
        return b

    identf = cload(ident_in, [128, 128])
    maskA = cload(maskA_in, [128, 128])
    maskB = cload(maskB_in, [128, 128])
    identb = K.sb("identb", [128, 128], BF16)
    K.op("dve", lambda e: e.tensor_copy(out=identb.t[:], in_=identf.t[:]), [identf()], [identb()])
    onesf = K.sb("onesf", [128, 128], F32)
    K.op("dve", lambda e: e.memset(onesf.t[:], 1.0), [], [onesf()])
    onesb = K.sb("onesb", [128, 128], BF16)
    K.op("dve", lambda e: e.memset(onesb.t[:], 1.0), [], [onesb()])

    cT_s = cload(cT, [128, 8, 2])
    bmodA_s = cload(bmodA, [128, 2, 32])
    n1g_s = cload(n1g, [128, 2, 8])
    n2g_s = cload(n2g, [128, 2, 8])
    bg_s = cload(bg, [128, 2])
    cw_s = cload(cw, [128, 4, 31])
    cb_s = cload(cb, [128, 4])
    lng_s = cload(lng, [128, 4])
    lnb_s = cload(lnb, [128, 4])
    mng_s = cload(mng, [128, 512])
    qg_s = cload(qg, [128, 128])
    kg_s = cload(kg, [128, 128])
    fng_s = cload(fng, [128, D])
    sel_s = cload(sel, [128, 2])
    rcos_s = cload(rcos, [128, NLT, 64])
    rsin_s = cload(rsin, [128, NLT, 64])


    PF = K.pool("pf", [128, 512], F32, 4, space="ps")
    PH = K.pool("ph", [128, 512], F32, 2, space="ps")
    PB = K.pool("pb", [128, 1024], BF16, 2, space="ps")

    scf = K.sb("scf", [128, 8, 2], F32)
    K.op("act", lambda e: e.activation(out=scf.t[:], in_=cT_s.t[:], func=AF.Silu), [cT_s()], [scf()])
    scb = K.sb("scb", [128, 8, 2], BF16)
    K.op("dve", lambda e: e.tensor_copy(out=scb.t[:], in_=scf.t[:]), [scf()], [scb()])
    AB1 = K.sb("AB", [128, 4, 8, 2], F32)
    AB = [AB1, AB1]
    Gb1 = [K.sb("Gb_%d" % j, [128, 2048], F32) for j in range(2)]
    Gb = [Gb1, Gb1]
    modA_s = K.sb("modA_s", [128, 32, 2], F32)

    def mod_layer(l):
      with K.scope():
        wmA_pool = K.pool("wmA", [128, 4, 1024], BF16, 2)
        wmG_pool = K.pool("wmG", [128, 8, 512], BF16, 2)
        bG_pool = K.pool("bGs", [128, 512], F32, 2)
        screp = [K.sb("screp%d" % j, [128, 8, 128], BF16) for j in range(2)]
        for j in range(2):
            K.op("dve", lambda e, j=j: e.tensor_copy(out=screp[j].t[:], in_=scf.t[:, :, j:j + 1].to_broadcast([128, 8, 128])),
                 [scf()], [screp[j]()])
        ps = PF.next()
        psv = ps.t[:, 0:64].rearrange("p (b j) -> p b j", j=2)
        for grp in range(8):
            wt = wmA_pool.next()
            K.dma("pool", wt(), wmodA.v(wmodA.t[l, grp * 4:(grp + 1) * 4].rearrange("b p k c -> p b (k c)")))
            for bi in range(4):
                blk = grp * 4 + bi
                for kt in range(8):
                    K.op("pe", lambda e, wt=wt, bi=bi, kt=kt, blk=blk: e.matmul(
                        psv[:, blk, :], lhsT=wt.t[:, bi, kt * 128:(kt + 1) * 128], rhs=scb.t[:, kt, :],
                        start=(kt == 0), stop=(kt == 7)), [wt(), scb()], [ps()])
        K.op("dve", lambda e: e.tensor_tensor(out=modA_s.t[:], in0=psv, in1=bmodA_s.t[:, l, :].unsqueeze(2).to_broadcast([128, 32, 2]), op=ALU.add),
             [ps(), bmodA_s()], [modA_s()])
        mv = modA_s.t[:].rearrange("p (f k) j -> p f k j", f=4)
        ab = AB[l]
        ng = (n1g_s, n2g_s)
        for which in range(2):
            sh = mv[:, 2 * which]
            sc_ = mv[:, 2 * which + 1]
            K.op("dve", lambda e, which=which, sc_=sc_: e.scalar_tensor_tensor(
                out=ab.t[:, 2 * which], in0=sc_, scalar=1.0, in1=ng[which].t[:, l, :].unsqueeze(2).to_broadcast([128, 8, 2]),
                op0=ALU.add, op1=ALU.mult), [modA_s(), ng[which]()], [ab(key=2 * which)])
            K.op("dve", lambda e, which=which, sh=sh: e.tensor_copy(out=ab.t[:, 2 * which + 1], in_=sh), [modA_s()], [ab(key=2 * which + 1)])
        for q in range(4):
            bgs = bG_pool.next()
            K.dma("sp", bgs(), bmodG(np.s_[l, :, q * 512:(q + 1) * 512]))
            wt = wmG_pool.next()
            K.dma("pool", wt(), wmodG.v(wmodG.t[l, q]))
            for j in range(2):
                ps2 = PF.next()
                for kt in range(8):
                    K.op("pe", lambda e, wt=wt, kt=kt, j=j, ps2=ps2: e.matmul(
                        ps2.t[:], lhsT=screp[j].t[:, kt, :], rhs=wt.t[:, kt, :], start=(kt == 0), stop=(kt == 7)),
                        [wt(), screp[j]()], [ps2()])
                K.op("dve", lambda e, j=j, q=q, ps2=ps2, bgs=bgs: e.tensor_tensor(
                    out=Gb[l][j].t[:, q * 512:(q + 1) * 512], in0=ps2.t[:], in1=bgs.t[:], op=ALU.add),
                    [ps2(), bgs()], [Gb[l][j](key=q)])

    mod_layer(0)
    cast_weight(w_ab_f, 6)
    cast_weight(w_ab_t, 3)
    cast_weight(w_oab, 2)
    cast_weight(w_fi, 2)
    cast_weight(w_fo, 2)
    cast_weight(w_c_q, 2)
    cast_weight(w_c_kv, 1)
    cast_weight(w_oc, 2)
    dump('AB', AB1); dump('Gb0', Gb1[0]); dump('Gb1', Gb1[1])
    if stage <= 1:
        return K, es, locals()

    xpool = K.pool("xt", [128, D], F32, 2)
    junk = K.pool("junk", [128, D], BF16, 1)
    sspool = K.pool("ss", [128, 2], F32, 4)
    xnpool = K.pool("xn", [128, D], BF16, 4)
    hTpool = K.pool("hT", [128, 8, 512], BF16, 1)

    def rms_rstd(xr, width, out_r):
        jk = junk.next()
        ss = sspool.next()
        K.op("act", lambda e: e.activation(out=jk.t[:, 0:width], in_=xr.ap, func=AF.Square, accum_out=ss.t[:, 0:1]),
             [xr], [jk(), ss()])
        K.op("act", lambda e: e.activation(out=ss.t[:, 1:2], in_=ss.t[:, 0:1], func=AF.Sqrt, scale=1.0 / width, bias=eps_s.t[:, 0:1]),
             [ss(), eps_s()], [ss()])
        K.op("dve", lambda e: e.reciprocal(out=out_r.ap, in_=ss.t[:, 1:2]), [ss()], [out_r])

    eps_s = K.sb("eps_s", [128, 1], F32)
    K.op("dve", lambda e: e.memset(eps_s.t[:], EPS), [], [eps_s()])
    rspool = K.pool("rs", [128, 1], F32, 6)

    def load_x(src, row0):
        xt = xpool.next()
        K.dma("sp", xt(), src(np.s_[row0:row0 + 128, :]))
        return xt

    def norm_mod_T(xtiles, l, which, j):
        n = len(xtiles)
        xns = []
        for xr in xtiles:
            if isinstance(xr, tuple):
                xr = load_x(xr[0], xr[1])()
            rs = rspool.next()
            rms_rstd(xr, D, rs())
            xn = xnpool.next()
            K.op("dve", lambda e, xr=xr, rs=rs, xn=xn: e.tensor_scalar(out=xn.t[:], in0=xr.ap, scalar1=rs.t[:, 0:1], scalar2=None, op0=ALU.mult),
                 [xr, rs()], [xn()])
            xns.append(xn)
        hT = hTpool.next()
        ab = AB[l]
        for kt in range(8):
            pb = PB.next()
            for i, xn in enumerate(xns):
                K.op("pe", lambda e, i=i, xn=xn, pb=pb, kt=kt: e.transpose(out=pb.t[:, i * 128:(i + 1) * 128], in_=xn.t[:, kt * 128:(kt + 1) * 128], identity=identb.t[:]),
                     [xn(), identb()], [pb()])
            K.op("act", lambda e, pb=pb, kt=kt, hT=hT: e.activation(
                out=hT.t[:, kt, 0:n * 128], in_=pb.t[:, 0:n * 128], func=AF.Identity,
                scale=ab.t[:, 2 * which, kt, j:j + 1], bias=ab.t[:, 2 * which + 1, kt, j:j + 1]),
                [pb(), ab(key=2 * which), ab(key=2 * which + 1)], [hT(key=kt)])
        return hT


    K.push_scope()
    ULAT = 16 + 2048 + 16
    UCTX = 16 + 256 + 16
    U = K.sb("U", [128, 4, ULAT + 2 * UCTX], BF16)
    K.op("pool", lambda e: e.memset(U.t[:], 0.0), [], [U()])

    def ucol(tok):
        if tok < 2048:
            return 16 + tok
        c = tok - 2048
        s = c // 256
        return ULAT + s * UCTX + 16 + (c % 256)

    GA = K.sb("GA", [128, NTOK], F32)
    GBt = K.sb("GBt", [128, NTOK], F32)
    K.push_scope()
    wfpool = K.pool("wabf", [128, 8, 128], BF16, 4)
    wtpool = K.pool("wabt", [128, 8, 512], BF16, 2)
    gtmp = K.pool("gtmp", [128, 512], F32, 2)
    evq = K.pool("evq", [128, 512], BF16, 4)
    evt = K.pool("evt", [128, 4, 129], BF16, 3)

    def fm_block(hT, n, wsrc, blk):
        wt = wfpool.next()
        K.dma("sp", wt(), WB[wsrc].v(WB[wsrc].t[blk], key=("w", blk // 3)))
        ps = PF.next()
        for kt in range(8):
            K.op("pe", lambda e, kt=kt, wt=wt, ps=ps: e.matmul(ps.t[:, 0:n], lhsT=wt.t[:, kt, :], rhs=hT.t[:, kt, 0:n], start=(kt == 0), stop=(kt == 7)),
                 [wt(), hT(key=kt)], [ps()])
        return ps

    def phaseA_group(tiles, j, halo=False):
        n = len(tiles) * 128
        tok0 = tiles[0] * 128
        hT = norm_mod_T([(xs, t * 128) for t in tiles], 0, 0, j)
        for c in range(4):
            psa = fm_block(hT, n, "w_ab_f", c)
            ga = gtmp.next()
            K.op("act", lambda e, psa=psa, ga=ga: e.activation(out=ga.t[:, 0:n], in_=psa.t[:, 0:n], func=AF.Copy), [psa()], [ga()])
            psg = fm_block(hT, n, "w_ab_f", 4 + c)
            sg = gtmp.next()
            K.op("act", lambda e, psg=psg, sg=sg: e.activation(out=sg.t[:, 0:n], in_=psg.t[:, 0:n], func=AF.Sigmoid), [psg()], [sg()])
            if halo:
                c0 = 16 + 2048
                segs = [(0, 16, c0)]
            elif tok0 < 2048:
                segs = [(0, n, ucol(tok0))]
            else:
                segs = [(0, 256, ucol(2048)), (256, 256, ucol(2048 + 256))]
            for (o0, ln, uc) in segs:
                K.op("dve", lambda e, ga=ga, sg=sg, c=c, o0=o0, ln=ln, uc=uc: e.tensor_tensor(
                    out=U.t[:, c, uc:uc + ln], in0=ga.t[:, o0:o0 + ln], in1=sg.t[:, o0:o0 + ln], op=ALU.mult),
                    [ga(), sg()], [U(key=(c, uc))])
        if halo:
            return
        for (dst, b0, scl) in ((S_qT, 8, 1.0), (S_kT, 12, 128 ** -0.5)):
            for h in range(4):
                ps = fm_block(hT, n, "w_ab_f", b0 + h)
                ev = evq.next()
                K.op("act", lambda e, ps=ps, ev=ev, scl=scl: e.activation(out=ev.t[:, 0:n], in_=ps.t[:, 0:n], func=AF.Copy, scale=scl), [ps()], [ev()])
                K.dma("pool", dst(np.s_[:, h, tok0:tok0 + n], key=tok0), ev(np.s_[:, 0:n]))
        for (dst, blk, col) in ((GA, 16, 0), (GBt, 17, 1)):
            ps = fm_block(hT, n, "w_ab_f", blk)
            K.op("act", lambda e, ps=ps, dst=dst, col=col: e.activation(out=dst.t[:, tok0:tok0 + n], in_=ps.t[:, 0:n], func=AF.Identity, bias=bg_s.t[:, col:col + 1]),
                 [ps(), bg_s()], [dst(key=tok0)])
        for wi, dst in enumerate((S_k, S_v, S_o)):
            wt = wtpool.next()
            K.dma("sp", wt(), WB["w_ab_t"].v(WB["w_ab_t"].t[wi], key=("w", wi)))
            for i, t in enumerate(tiles):
                ps = PF.next()
                for kt in range(8):
                    K.op("pe", lambda e, kt=kt, wt=wt, ps=ps, i=i: e.matmul(ps.t[:], lhsT=hT.t[:, kt, i * 128:(i + 1) * 128], rhs=wt.t[:, kt, :], start=(kt == 0), stop=(kt == 7)),
                         [wt(), hT(key=kt)], [ps()])
                if wi == 1:
                    ev = evt.next()
                    K.op("dve", lambda e, ev=ev: e.memset(ev.t[:, :, 128:129], 1.0), [], [ev()])
                    K.op("act", lambda e, ps=ps, ev=ev: e.activation(out=ev.t[:, :, 0:128], in_=ps.t[:].rearrange("p (h c) -> p h c", h=4), func=AF.Copy), [ps()], [ev()])
                    K.dma("pool", dst(np.s_[t * 128:(t + 1) * 128], key=t), ev())
                else:
                    ev = evq.next()
                    K.op("act", lambda e, ps=ps, ev=ev, wi=wi: e.activation(out=ev.t[:], in_=ps.t[:], func=AF.Copy, scale=(128 ** -0.5 if wi == 0 else 1.0)), [ps()], [ev()])
                    K.dma("pool", dst(np.s_[t * 128:(t + 1) * 128, :], key=t), ev())

    for g in range(5):
        phaseA_group(list(range(4 * g, 4 * g + 4)), 0 if g < 4 else 1)
    phaseA_group([20], 0, halo=True)
    K.pop_scope()
    dump('U', U); dump('GA', GA); dump('GBt', GBt)
    if stage <= 2:
        K.pop_scope()
        return K, es, locals()


    K.push_scope()
    Dg = K.sb("Dg", [128, 4, 31, 128], BF16)
    for c in range(4):
        for jj in range(31):
            K.op("dve", lambda e, c=c, jj=jj: e.tensor_scalar(out=Dg.t[:, c, jj, :], in0=identf.t[:], scalar1=cw_s.t[:, c, jj:jj + 1], scalar2=None, op0=ALU.mult),
                 [identf(), cw_s()], [Dg(key=(c, jj))])
    ucp = K.pool("ucp", [128, 512], F32, 5)
    lnp = K.pool("lnp", [128, 512], F32, 12)
    aop = K.pool("aop", [128, 512], BF16, 3)

    def conv_group(tok0):
        if tok0 < 2048:
            segs = [(0, 512, ucol(tok0))]
        else:
            segs = [(0, 256, ucol(2048)), (256, 256, ucol(2048 + 256))]
        ucs = []
        for c in range(4):
            ps = PF.next()
            for (o0, ln, uc) in segs:
                for jj in range(31):
                    K.op("pe", lambda e, c=c, jj=jj, ps=ps, o0=o0, ln=ln, uc=uc: e.matmul(
                        ps.t[:, o0:o0 + ln], lhsT=Dg.t[:, c, jj, :], rhs=U.t[:, c, uc + jj - 15:uc + jj - 15 + ln], start=(jj == 0), stop=(jj == 30)),
                        [Dg(), U()], [ps()])
            ucb = ucp.next()
            K.op("act", lambda e, ps=ps, ucb=ucb, c=c: e.activation(out=ucb.t[:], in_=ps.t[:], func=AF.Identity, bias=cb_s.t[:, c:c + 1]), [ps(), cb_s()], [ucb()])
            ucs.append(ucb)
        ps_s = PF.next()
        ps_q = PF.next()
        for c in range(4):
            K.op("pe", lambda e, c=c: e.matmul(ps_s.t[:], lhsT=onesf.t[:], rhs=ucs[c].t[:], start=(c == 0), stop=(c == 3)), [onesf(), ucs[c]()], [ps_s()])
        for c in range(4):
            sq = lnp.next()
            K.op("act", lambda e, c=c, sq=sq: e.activation(out=sq.t[:], in_=ucs[c].t[:], func=AF.Square), [ucs[c]()], [sq()])
            K.op("pe", lambda e, c=c, sq=sq: e.matmul(ps_q.t[:], lhsT=onesf.t[:], rhs=sq.t[:], start=(c == 0), stop=(c == 3)), [onesf(), sq()], [ps_q()])
        mean = lnp.next()
        K.op("act", lambda e: e.activation(out=mean.t[:], in_=ps_s.t[:], func=AF.Copy, scale=1.0 / 512), [ps_s()], [mean()])
        msq = lnp.next()
        K.op("dve", lambda e: e.tensor_tensor(out=msq.t[:], in0=mean.t[:], in1=mean.t[:], op=ALU.mult), [mean()], [msq()])
        var = lnp.next()
        K.op("dve", lambda e: e.scalar_tensor_tensor(out=var.t[:], in0=ps_q.t[:], scalar=1.0 / 512, in1=msq.t[:], op0=ALU.mult, op1=ALU.subtract), [ps_q(), msq()], [var()])
        K.op("act", lambda e: e.activation(out=var.t[:], in_=var.t[:], func=AF.Sqrt, bias=eps_s.t[:, 0:1]), [var(), eps_s()], [var()])
        rstd = lnp.next()
        K.op("dve", lambda e: e.reciprocal(out=rstd.t[:], in_=var.t[:]), [var()], [rstd()])
        for c in range(4):
            t1 = lnp.next()
            K.op("dve", lambda e, c=c, t1=t1: e.tensor_tensor(out=t1.t[:], in0=ucs[c].t[:], in1=mean.t[:], op=ALU.subtract), [ucs[c](), mean()], [t1()])
            K.op("dve", lambda e, t1=t1: e.tensor_tensor(out=t1.t[:], in0=t1.t[:], in1=rstd.t[:], op=ALU.mult), [t1(), rstd()], [t1()])
            ao = aop.next()
            K.op("act", lambda e, c=c, t1=t1, ao=ao: e.activation(out=ao.t[:], in_=t1.t[:], func=AF.Silu, scale=lng_s.t[:, c:c + 1], bias=lnb_s.t[:, c:c + 1]),
                 [t1(), lng_s(), lnb_s()], [ao()])
            K.dma("pool", S_act(np.s_[:, c, tok0:tok0 + 512], key=("u", tok0)), ao())

    for g in range(5):
        conv_group(g * 512)
    K.pop_scope()
    if stage <= 3:
        K.pop_scope()
        return K, es, locals()

    K.push_scope()
    NCH = 40
    rmask = K.sb("rmask", [4, NTOK], F32)
    K.op("dve", lambda e: e.memset(rmask.t[:], 1.0), [], [rmask()])
    K.op("dve", lambda e: e.memset(rmask.t[:].rearrange("p (c t) -> p c t", t=64)[:, :, 0:1], 0.0), [], [rmask()])
    gp = {n_: K.sb("gp_" + n_, [4, NTOK], F32) for n_ in ("lp", "cs", "bn", "a")}
    gp["wk"] = gp["a"]
    gp["e"] = gp["bn"]
    gs = {n_: K.sb("gs_" + n_, [4, NCH], F32) for n_ in ("amax", "Mc", "m", "mprev", "g", "tmp")}
    Gd = K.sb("Gd", [4, 4, NCH], F32)
    WE = K.sb("WE", [128, NTT, 8], F32)
    gB = K.sb("gB", [128, 4, NCH], F32)
    zero4 = K.sb("zero4", [128, 4], F32)
    K.op("dve", lambda e: e.memset(zero4.t[:], 0.0), [], [zero4()])
    mlat = K.sb("mlat", [4, 1], F32)
    K.dma("sp", mlat(), st_m())
    Cext = K.sb("Cext", [128, 4, 129], F32)
    Cgp = K.pool("Cg", [128, 4, 129], BF16, 3)
    qzp = [(K.sb("qz0_%d" % i, [128, 4, 128], BF16), K.sb("qz1_%d" % i, [128, 4, 128], BF16)) for i in range(2)]
    for (a_, b_) in qzp:
        K.op("dve", lambda e, a_=a_: e.memset(a_.t[:], 0.0), [], [a_()])
        K.op("dve", lambda e, b_=b_: e.memset(b_.t[:], 0.0), [], [b_()])
    kTp = K.pool("kTt", [128, 4, 128], BF16, 2)
    ktp = K.pool("ktt", [128, 512], BF16, 2)
    vtp = K.pool("vtt", [128, 4, 129], BF16, 2)
    Smp = K.pool("Sm", [128, 4, 128], BF16, 2)
    Vpp = K.pool("Vp", [128, 4, 129], BF16, 2)
    hhp = K.pool("hh", [128, 4, 128], F32, 2)
    s4p = K.pool("s4", [128, 8], F32, 4)
    otp = K.pool("ot", [128, 512], BF16, 2)
    hfp = K.pool("hft", [128, 512], F32, 2)
    hmp = K.pool("hm", [128, 512], BF16, 2)
    hmT = K.pool("hmT", [128, 4, 128], BF16, 2)
    stg = K.pool("stg", [128, 520], F32, 2)

    groups = [(list(range(0, 16)), "lat"), ([16, 17], 0), ([18, 19], 1)]

    def gates_prep(G, pas):
        i_ = G.t[0:4, :]
        f_ = G.t[32:36, :]
        lp, cs, bn, a, wk, ee = (gp[n_] for n_ in ("lp", "cs", "bn", "a", "wk", "e"))
        K.op("act", lambda e: e.activation(out=lp.t[:], in_=f_, func=AF.Exp, scale=-1.0), [G()], [lp()])
        K.op("act", lambda e: e.activation(out=lp.t[:], in_=lp.t[:], func=AF.Ln, bias=onesf.t[0:4, 0:1]), [lp(), onesf()], [lp()])
        K.op("dve", lambda e: e.tensor_tensor_scan(out=cs.t[:], data0=rmask.t[:], data1=lp.t[:], initial=0.0, op0=ALU.mult, op1=ALU.add), [rmask(), lp()], [cs()])
        c3 = lambda b_: b_.t[:].rearrange("p (c t) -> p c t", t=64)
        if pas == 0:
            K.op("dve", lambda e: e.tensor_copy(out=bn.t[:], in_=cs.t[:]), [cs()], [bn()])
            bend = c3(cs)[:, :, 63]
        else:
            K.op("dve", lambda e: e.tensor_tensor(out=c3(bn), in0=c3(lp), in1=c3(cs), op=ALU.subtract), [lp(), cs()], [bn()])
            K.op("dve", lambda e: e.tensor_tensor(out=c3(bn), in0=c3(bn), in1=c3(cs)[:, :, 63:64].to_broadcast([4, NCH, 64]), op=ALU.add), [bn(), cs()], [bn()])
            K.op("dve", lambda e: e.tensor_copy(out=cs.t[:], in_=bn.t[:]), [bn()], [cs()])
            bend = c3(cs)[:, :, 0]
        K.op("dve", lambda e: e.tensor_tensor(out=a.t[:], in0=i_, in1=bn.t[:], op=ALU.add), [G(), bn()], [a()])
        K.op("dve", lambda e: e.tensor_reduce(out=gs["amax"].t[:], in_=c3(a), axis=AX.X, op=ALU.max), [a()], [gs["amax"]()])
        for (tiles, kind) in groups:
            chs = list(range(tiles[0] * 2, tiles[-1] * 2 + 2))
            if pas == 1:
                chs = chs[::-1]
            for ci, c in enumerate(chs):
                if ci == 0:
                    prev = mlat.t[:, 0:1] if kind == "lat" else zero4.t[0:4, 0:1]
                    prd = [mlat()] if kind == "lat" else [zero4()]
                else:
                    pc = chs[ci - 1]
                    prev = gs["m"].t[:, pc:pc + 1]
                    prd = [gs["m"]()]
                K.op("dve", lambda e, c=c, prev=prev: e.tensor_copy(out=gs["mprev"].t[:, c:c + 1], in_=prev), prd, [gs["mprev"]()])
                K.op("dve", lambda e, c=c, prev=prev: e.tensor_tensor(out=gs["Mc"].t[:, c:c + 1], in0=prev, in1=gs["amax"].t[:, c:c + 1], op=ALU.max), prd + [gs["amax"]()], [gs["Mc"]()])
                K.op("dve", lambda e, c=c: e.tensor_tensor(out=gs["m"].t[:, c:c + 1], in0=gs["Mc"].t[:, c:c + 1], in1=bend[:, c:c + 1], op=ALU.subtract), [gs["Mc"](), cs()], [gs["m"]()])
        K.op("dve", lambda e: e.tensor_tensor(out=gs["tmp"].t[:], in0=gs["mprev"].t[:], in1=gs["Mc"].t[:], op=ALU.subtract), [gs["mprev"](), gs["Mc"]()], [gs["tmp"]()])
        K.op("act", lambda e: e.activation(out=gs["g"].t[:], in_=gs["tmp"].t[:], func=AF.Exp), [gs["tmp"]()], [gs["g"]()])
        mcb = gs["Mc"].t[:].unsqueeze(2).to_broadcast([4, NCH, 64])
        K.op("dve", lambda e: e.tensor_tensor(out=c3(wk), in0=c3(a), in1=mcb, op=ALU.subtract), [a(), gs["Mc"]()], [wk()])
        K.op("act", lambda e: e.activation(out=wk.t[:], in_=wk.t[:], func=AF.Exp), [wk()], [wk()])
        K.op("dve", lambda e: e.tensor_tensor(out=c3(ee), in0=c3(bn), in1=mcb, op=ALU.subtract), [bn(), gs["Mc"]()], [ee()])
        K.op("act", lambda e: e.activation(out=ee.t[:], in_=ee.t[:], func=AF.Exp), [ee()], [ee()])
        ps = PF.next()
        psv = ps.t[:, 0:NTT * 8].rearrange("p (t k) -> p t k", k=8)
        for t in range(NTT):
            K.op("pe", lambda e, t=t: e.matmul(psv[:, t, 0:4], lhsT=wk.t[:, t * 128:(t + 1) * 128], rhs=identf.t[0:4, 0:4], start=True, stop=True), [wk(), identf()], [ps()])
            K.op("pe", lambda e, t=t: e.matmul(psv[:, t, 4:8], lhsT=ee.t[:, t * 128:(t + 1) * 128], rhs=identf.t[0:4, 0:4], start=True, stop=True), [ee(), identf()], [ps()])
        K.op("dve", lambda e: e.tensor_copy(out=WE.t[:], in_=psv), [ps()], [WE()])
        K.op("dve", lambda e: e.tensor_tensor(out=Gd.t[:], in0=identf.t[0:4, 0:4].unsqueeze(2).to_broadcast([4, 4, NCH]),
                                              in1=gs["g"].t[:].unsqueeze(1).to_broadcast([4, 4, NCH]), op=ALU.mult), [identf(), gs["g"]()], [Gd()])
        ps2 = PF.next()
        K.op("pe", lambda e: e.matmul(ps2.t[:, 0:4 * NCH], lhsT=onesf.t[0:4, :], rhs=Gd.t[:].rearrange("p a c -> p (a c)"), start=True, stop=True), [onesf(), Gd()], [ps2()])
        K.op("dve", lambda e: e.tensor_copy(out=gB.t[:].rearrange("p a c -> p (a c)"), in_=ps2.t[:, 0:4 * NCH]), [ps2()], [gB()])

    def mlstm_pass(pas):
        mask = maskA if pas == 0 else maskB
        tcount = 0
        for (tiles, kind) in groups:
            order = tiles if pas == 0 else tiles[::-1]
            if kind == "lat":
                if pas == 0:
                    K.dma("sp", Cext(), st_C())
            else:
                K.op("dve", lambda e: e.memset(Cext.t[:], 0.0), [], [Cext()])
            fc = order[0] * 2 + (0 if pas == 0 else 1)
            Cg = Cgp.next()
            K.op("dve", lambda e, Cg=Cg, fc=fc: e.tensor_tensor(out=Cg.t[:], in0=Cext.t[:], in1=gB.t[:, :, fc:fc + 1].to_broadcast([128, 4, 129]), op=ALU.mult), [Cext(), gB()], [Cg()])
            for ti, t in enumerate(order):
                tok0 = t * 128
                qz0, qz1 = qzp[tcount % 2]
                tcount += 1
                K.dma("sp", qz0(np.s_[:, :, 0:64]), S_qT(np.s_[:, :, tok0:tok0 + 64]))
                K.dma("sp", qz1(np.s_[:, :, 64:128]), S_qT(np.s_[:, :, tok0 + 64:tok0 + 128]))
                kT = kTp.next(); K.dma("sp", kT(), S_kT(np.s_[:, :, tok0:tok0 + 128]))
                kt_ = ktp.next(); K.dma("sp", kt_(), S_k(np.s_[tok0:tok0 + 128, :]))
                vt = vtp.next(); K.dma("sp", vt(), S_v(np.s_[tok0:tok0 + 128]))
                psS = PF.next()
                sv = psS.t[:].rearrange("p (h t) -> p h t", h=4)
                for h in range(4):
                    K.op("pe", lambda e, h=h, kT=kT, qz0=qz0: e.matmul(sv[:, h, 0:64], lhsT=kT.t[:, h, :], rhs=qz0.t[:, h, 0:64], start=True, stop=True), [kT(), qz0()], [psS()])
                    K.op("pe", lambda e, h=h, kT=kT, qz1=qz1: e.matmul(sv[:, h, 64:128], lhsT=kT.t[:, h, :], rhs=qz1.t[:, h, 64:128], start=True, stop=True), [kT(), qz1()], [psS()])
                Sm = Smp.next()
                K.op("dve", lambda e, Sm=Sm, sv=sv: e.tensor_tensor(out=Sm.t[:], in0=sv, in1=mask.t[:].unsqueeze(1).to_broadcast([128, 4, 128]), op=ALU.mult), [psS(), mask()], [Sm()])
                Vp = Vpp.next()
                K.op("dve", lambda e, Vp=Vp, vt=vt, t=t: e.tensor_tensor(out=Vp.t[:], in0=vt.t[:], in1=WE.t[:, t, 0:4].unsqueeze(2).to_broadcast([128, 4, 129]), op=ALU.mult), [vt(), WE()], [Vp()])
                nd = [PH.next(), PH.next()]
                ndv = [b_.t[:, 0:258].rearrange("p (h c) -> p h c", h=2) for b_ in nd]
                halves = [(0, 64), (64, 128)] if pas == 0 else [(64, 128), (0, 64)]
                qzs = [qz0, qz1] if pas == 0 else [qz1, qz0]
                for h in range(4):
                    K.op("pe", lambda e, h=h, Sm=Sm, Vp=Vp: e.matmul(ndv[h // 2][:, h % 2, :], lhsT=Sm.t[:, h, :], rhs=Vp.t[:, h, :], start=(h % 2 == 0), stop=False), [Sm(), Vp()], [nd[h // 2]()])
                for half in range(2):
                    r0, r1 = halves[half]
                    cidx = t * 2 + (0 if r0 == 0 else 1)
                    qz = qzs[half]
                    for h in range(4):
                        K.op("pe", lambda e, h=h, qz=qz, Cg=Cg, half=half: e.matmul(ndv[h // 2][:, h % 2, :], lhsT=qz.t[:, h, :], rhs=Cg.t[:, h, :], start=False, stop=(half == 1 and h % 2 == 1)), [qz(), Cg()], [nd[h // 2]()])
                    dC = [PF.next(), PF.next()]
                    dCv = [b_.t[:, 0:258].rearrange("p (h c) -> p h c", h=2) for b_ in dC]
                    for h in range(4):
                        K.op("pe", lambda e, h=h, kt_=kt_, Vp=Vp, r0=r0, r1=r1, dCv=dCv: e.matmul(dCv[h // 2][:, h % 2, :], lhsT=kt_.t[r0:r1, h * 128:(h + 1) * 128], rhs=Vp.t[r0:r1, h, :], start=True, stop=True), [kt_(), Vp()], [dC[h // 2]()])
                    for h in range(4):
                        K.op("dve", lambda e, h=h, cidx=cidx, dCv=dCv: e.scalar_tensor_tensor(out=Cext.t[:, h, :], in0=Cext.t[:, h, :], scalar=gB.t[:, h, cidx:cidx + 1], in1=dCv[h // 2][:, h % 2, :], op0=ALU.mult, op1=ALU.add),
                             [Cext(), gB(), dC[h // 2]()], [Cext()])
                    if half == 0:
                        nxt = t * 2 + (1 if r0 == 0 else 0)
                    elif ti + 1 < len(order):
                        nxt = order[ti + 1] * 2 + (0 if pas == 0 else 1)
                    else:
                        nxt = None
                    if nxt is not None:
                        Cg = Cgp.next()
                        K.op("dve", lambda e, Cg=Cg, nxt=nxt: e.tensor_tensor(out=Cg.t[:], in0=Cext.t[:], in1=gB.t[:, :, nxt:nxt + 1].to_broadcast([128, 4, 129]), op=ALU.mult), [Cext(), gB()], [Cg()])
                s4 = s4p.next()
                for b2 in range(2):
                    K.op("act", lambda e, b2=b2, s4=s4: e.activation(out=s4.t[:, 2 * b2:2 * b2 + 2], in_=ndv[b2][:, :, 128], func=AF.Abs), [nd[b2]()], [s4()])
                    K.op("dve", lambda e, b2=b2, s4=s4, t=t: e.tensor_tensor(out=s4.t[:, 2 * b2:2 * b2 + 2], in0=s4.t[:, 2 * b2:2 * b2 + 2], in1=WE.t[:, t, 4 + 2 * b2:6 + 2 * b2], op=ALU.max),
                         [s4(), WE()], [s4()])
                K.op("dve", lambda e, s4=s4: e.reciprocal(out=s4.t[:, 4:8], in_=s4.t[:, 0:4]), [s4()], [s4()])
                hh = hhp.next()
                for b2 in range(2):
                    K.op("dve", lambda e, b2=b2, s4=s4, hh=hh: e.tensor_tensor(out=hh.t[:, 2 * b2:2 * b2 + 2, :], in0=ndv[b2][:, :, 0:128], in1=s4.t[:, 4 + 2 * b2:6 + 2 * b2].unsqueeze(2).to_broadcast([128, 2, 128]), op=ALU.mult),
                         [nd[b2](), s4()], [hh()])
                if pas == 0:
                    K.dma("pool", S_hf(np.s_[tok0:tok0 + 128, :], key=t), hh.v(hh.t[:].rearrange("p h c -> p (h c)")))
                else:
                    hf = hfp.next(); K.dma("sp", hf(), S_hf(np.s_[tok0:tok0 + 128, :], key=t))
                    ot = otp.next(); K.dma("sp", ot(), S_o(np.s_[tok0:tok0 + 128, :], key=t))
                    hs = hh.t[:].rearrange("p h c -> p (h c)")
                    K.op("dve", lambda e, hs=hs, hf=hf: e.tensor_tensor(out=hs, in0=hs, in1=hf.t[:], op=ALU.add), [hh(), hf()], [hh()])
                    K.op("dve", lambda e, hs=hs, hf=hf: e.tensor_tensor(out=hf.t[:], in0=hs, in1=hs, op=ALU.mult), [hh()], [hf()])
                    s5 = s4p.next()
                    K.op("dve", lambda e, s5=s5, hf=hf: e.tensor_reduce(out=s5.t[:, 0:4], in_=hf.t[:].rearrange("p (h c) -> p h c", h=4), axis=AX.X, op=ALU.add), [hf()], [s5()])
                    K.op("act", lambda e, s5=s5: e.activation(out=s5.t[:, 0:4], in_=s5.t[:, 0:4], func=AF.Sqrt, scale=1.0 / 128, bias=eps_s.t[:, 0:1]), [s5(), eps_s()], [s5()])
                    K.op("dve", lambda e, s5=s5: e.reciprocal(out=s5.t[:, 4:8], in_=s5.t[:, 0:4]), [s5()], [s5()])
                    K.op("dve", lambda e, s5=s5, hh=hh: e.tensor_tensor(out=hh.t[:], in0=hh.t[:], in1=s5.t[:, 4:8].unsqueeze(2).to_broadcast([128, 4, 128]), op=ALU.mult), [hh(), s5()], [hh()])
                    K.op("dve", lambda e, hs=hs: e.tensor_tensor(out=hs, in0=hs, in1=mng_s.t[:], op=ALU.mult), [hh(), mng_s()], [hh()])
                    K.op("act", lambda e, hf=hf, ot=ot: e.activation(out=hf.t[:], in_=ot.t[:], func=AF.Sigmoid), [ot()], [hf()])
                    hm = hmp.next()
                    K.op("dve", lambda e, hs=hs, hf=hf, hm=hm: e.tensor_tensor(out=hm.t[:], in0=hs, in1=hf.t[:], op=ALU.mult), [hh(), hf()], [hm()])
                    pb = PB.next()
                    for h in range(4):
                        K.op("pe", lambda e, h=h, hm=hm, pb=pb: e.transpose(out=pb.t[:, h * 128:(h + 1) * 128], in_=hm.t[:, h * 128:(h + 1) * 128], identity=identb.t[:]), [hm(), identb()], [pb()])
                    hT_ = hmT.next()
                    K.op("act", lambda e, pb=pb, hT_=hT_: e.activation(out=hT_.t[:].rearrange("p h c -> p (h c)"), in_=pb.t[:, 0:512], func=AF.Copy), [pb()], [hT_()])
                    K.dma("pool", S_act(np.s_[:, 4:8, tok0:tok0 + 128], key=("h", t)), hT_())
            lastc = order[-1] * 2 + (1 if pas == 0 else 0)
            if kind == "lat":
                if pas == 0:
                    K.dma("pool", S_st(np.s_[:, 516:520], key="z"), zero4())
                    K.dma("pool", S_st(np.s_[:, 0:516], key="c"), Cext.v(Cext.t[:].rearrange("p h c -> p (h c)")))
                    K.dma("pool", S_st(np.s_[0:4, 516:517], key="z"), gs["m"](np.s_[:, lastc:lastc + 1]), allow_slow_non_contiguous=True)
            else:
                sq = kind
                K.dma("pool", oC.v(oC.t[sq, pas].rearrange("h d v -> d h v")), Cext(np.s_[:, :, 0:128]))
                K.dma("pool", on.v(on.t[sq, pas].rearrange("h (d o) -> d h o", o=1)), Cext(np.s_[:, :, 128:129]), allow_slow_non_contiguous=True)
                K.dma("pool", om.v(om.t[sq, pas].rearrange("(h o) -> h o", o=1)), gs["m"](np.s_[:, lastc:lastc + 1]), allow_slow_non_contiguous=True)

    gates_prep(GA, 0)
    dump('WE', WE); dump('gB', gB)
    if stage <= 4:
        K.pop_scope(); K.pop_scope()
        return K, es, locals()
    mlstm_pass(0)
    if stage <= 5:
        K.pop_scope(); K.pop_scope()
        return K, es, locals()
    K.cc(lambda e: e.collective_compute("AllGather", ALU.bypass, replica_groups=[[0, 1], [2, 3], [4, 5], [6, 7]], ins=[S_st.t[:, :]], outs=[S_stg.t[:, :]]),
         [S_st()], [S_stg()])
    g0 = stg.next(); K.dma("sp", g0(), S_stg(np.s_[0:128, :]))
    g1 = stg.next(); K.dma("sp", g1(), S_stg(np.s_[128:256, :]))
    K.op("dve", lambda e: e.tensor_scalar(out=g1.t[:], in0=g1.t[:], scalar1=sel_s.t[:, 1:2], scalar2=None, op0=ALU.mult), [g1(), sel_s()], [g1()])
    K.op("dve", lambda e: e.scalar_tensor_tensor(out=g0.t[:], in0=g0.t[:], scalar=sel_s.t[:, 0:1], in1=g1.t[:], op0=ALU.mult, op1=ALU.add), [g0(), g1(), sel_s()], [g0()])
    K.op("dve", lambda e: e.tensor_copy(out=Cext.t[:].rearrange("p h c -> p (h c)"), in_=g0.t[:, 0:516]), [g0()], [Cext()])
    K.op("dve", lambda e: e.tensor_copy(out=mlat.t[:], in_=g0.t[0:4, 516:517]), [g0()], [mlat()])
    if stage <= 6:
        K.pop_scope(); K.pop_scope()
        return K, es, locals()
    gates_prep(GBt, 1)
    mlstm_pass(1)
    K.pop_scope()
    K.pop_scope()

    def post_scope_alloc(need_aT=True):
        P = {}
        P["wo"] = [K.sb("wo%d" % i, [128, 8, 512], BF16) for i in range(2)]
        if need_aT:
            P["aT"] = K.pool("aT", [128, 8, 512], BF16, 2)
        P["x1"] = K.pool("x1", [128, D], F32, 4)
        P["tmp"] = K.pool("ptmp", [128, 512], F32, 2)
        P["wfi"] = K.pool("wfi", [128, 8, 256], BF16, 2)
        P["wfo"] = K.pool("wfo", [128, 2, 512], BF16, 3)
        P["actF"] = K.pool("actF", [128, NFB, 512], BF16, 1)
        P["sg"] = K.pool("sg", [128, 512], F32, 1)
        return P

    def load_wo(P, name):
        for half in range(2):
            K.dma("sp", P["wo"][half](), WB[name].v(WB[name].t[half], key=("w", half)))

    def outproj_residual(P, l, j, aT, i, xin):
        x1 = P["x1"].next()
        for half in range(2):
            ps = PF.next()
            for kt in range(8):
                K.op("pe", lambda e, kt=kt, ps=ps, half=half: e.matmul(ps.t[:], lhsT=aT.t[:, kt, i * 128:(i + 1) * 128], rhs=P["wo"][half].t[:, kt, :], start=(kt == 0), stop=(kt == 7)),
                     [aT(), P["wo"][half]()], [ps()])
            tm = P["tmp"].next()
            K.op("dve", lambda e, ps=ps, tm=tm, half=half: e.tensor_tensor(out=tm.t[:], in0=ps.t[:], in1=Gb[l][j].t[:, half * 512:(half + 1) * 512], op=ALU.mult), [ps(), Gb[l][j]()], [tm()])
            K.op("dve", lambda e, tm=tm, half=half, x1=x1: e.tensor_tensor(out=x1.t[:, half * 512:(half + 1) * 512], in0=tm.t[:], in1=xin.ap[:, half * 512:(half + 1) * 512], op=ALU.add), [tm(), xin], [x1(key=half)])
        return x1

    def ffn_group(P, l, j, x1s, tok0, final):
        hT2 = norm_mod_T([x() for x in x1s], l, 1, j)
        actF = P["actF"].next()
        for jb in range(NFB):
            wt = P["wfi"].next()
            K.dma("sp", wt(), WB["w_fi"].v(WB["w_fi"].t[l, jb], key=("w", l)))
            psg = PF.next()
            psu = PF.next()
            for kt in range(8):
                K.op("pe", lambda e, kt=kt, wt=wt, psg=psg: e.matmul(psg.t[:], lhsT=wt.t[:, kt, 0:128], rhs=hT2.t[:, kt, :], start=(kt == 0), stop=(kt == 7)), [wt(), hT2(key=kt)], [psg()])
            for kt in range(8):
                K.op("pe", lambda e, kt=kt, wt=wt, psu=psu: e.matmul(psu.t[:], lhsT=wt.t[:, kt, 128:256], rhs=hT2.t[:, kt, :], start=(kt == 0), stop=(kt == 7)), [wt(), hT2(key=kt)], [psu()])
            sg = P["sg"].next()
            K.op("act", lambda e, psg=psg, sg=sg: e.activation(out=sg.t[:], in_=psg.t[:], func=AF.Silu), [psg()], [sg()])
            K.op("dve", lambda e, psu=psu, sg=sg, jb=jb: e.tensor_tensor(out=actF.t[:, jb, :], in0=psu.t[:], in1=sg.t[:], op=ALU.mult), [psu(), sg()], [actF(key=jb)])
        for half in range(2):
            pss = [PH.next(), PH.next(), PF.next(), PF.next()]
            for jc in range(11):
                wt = P["wfo"].next()
                K.dma("sp", wt(), WB["w_fo"].v(WB["w_fo"].t[l, half, :, jc * 2:(jc + 1) * 2, :], key=("w", l)))
                for i in range(4):
                    for jj in range(2):
                        jb = jc * 2 + jj
                        K.op("pe", lambda e, i=i, jj=jj, jb=jb, wt=wt: e.matmul(pss[i].t[:], lhsT=actF.t[:, jb, i * 128:(i + 1) * 128], rhs=wt.t[:, jj, :], start=(jb == 0), stop=(jb == NFB - 1)),
                             [actF(key=jb), wt()], [pss[i]()])
            for i in range(4):
                tm = P["tmp"].next()
                K.op("dve", lambda e, i=i, tm=tm, half=half: e.tensor_tensor(out=tm.t[:], in0=pss[i].t[:], in1=Gb[l][j].t[:, 1024 + half * 512:1024 + (half + 1) * 512], op=ALU.mult), [pss[i](), Gb[l][j]()], [tm()])
                K.op("dve", lambda e, i=i, tm=tm, half=half: e.tensor_tensor(out=x1s[i].t[:, half * 512:(half + 1) * 512], in0=tm.t[:], in1=x1s[i].t[:, half * 512:(half + 1) * 512], op=ALU.add), [tm(), x1s[i](key=half)], [x1s[i](key=half)])
        for i in range(4):
            r0 = tok0 + i * 128
            if not final:
                K.dma("pool", S_x1(np.s_[r0:r0 + 128, :], key=r0), x1s[i]())
            else:
                rs = rspool.next()
                rms_rstd(x1s[i](), D, rs())
                K.op("dve", lambda e, i=i, rs=rs: e.scalar_tensor_tensor(out=x1s[i].t[:], in0=x1s[i].t[:], scalar=rs.t[:, 0:1], in1=fng_s.t[:], op0=ALU.mult, op1=ALU.mult), [x1s[i](), rs(), fng_s()], [x1s[i]()])
                K.dma("pool", y(np.s_[r0:r0 + 128, :]), x1s[i]())

    if stage <= 7:
        return K, es, locals()
    K.push_scope()
    P = post_scope_alloc()
    load_wo(P, "w_oab")
    for g in range(5):
        j = 0 if g < 4 else 1
        tok0 = g * 512
        aT = P["aT"].next()
        K.dma("sp", aT(), S_act(np.s_[:, :, tok0:tok0 + 512]))
        x1s = []
        for i in range(4):
            xt = load_x(xs, tok0 + i * 128)
            x1s.append(outproj_residual(P, 0, j, aT, i, xt()))
        ffn_group(P, 0, j, x1s, tok0, final=False)
    K.pop_scope()

    if stage <= 8:
        return K, es, locals()
    mod_layer(1)
    S_ckT = K.dram("S_ckT", [128, 2, 512], BF16)
    S_cv = K.dram("S_cv", [512, 2, 128], BF16)

    def head_norm(src3, nh, gain, out3, P):
        sq = P["hn_sq"].next()
        sqv = sq.t[:, 0:nh * 128].rearrange("p (h c) -> p h c", h=nh)
        s8 = P["hn_s"].next()
        return sq, sqv, s8

    def rope(P, xq, nh, t, out_bf):
        xv = xq.t[:, 0:nh * 128].rearrange("p (h r a c) -> p h r a c", h=nh, r=2, a=2)
        ov = out_bf.t[:, 0:nh * 128].rearrange("p (h r a c) -> p h r a c", h=nh, r=2, a=2)
        A_ = P["ropeA"].next()
        B_ = P["ropeB"].next()
        av = A_.t[:, 0:nh * 128].rearrange("p (h r a c) -> p h r a c", h=nh, r=2, a=2)
        bv = B_.t[:, 0:nh * 128].rearrange("p (h r a c) -> p h r a c", h=nh, r=2, a=2)
        cosb = rcos_s.t[:, t, :].rearrange("p (r c) -> p r c", r=2)
        sinb = rsin_s.t[:, t, :].rearrange("p (r c) -> p r c", r=2)
        for a in range(2):
            K.op("dve", lambda e, a=a: e.tensor_tensor(out=av[:, :, :, a, :], in0=xv[:, :, :, a, :], in1=cosb.unsqueeze(1).to_broadcast([128, nh, 2, 32]), op=ALU.mult), [xq(), rcos_s()], [A_()])
            K.op("dve", lambda e, a=a: e.tensor_tensor(out=bv[:, :, :, a, :], in0=xv[:, :, :, 1 - a, :], in1=sinb.unsqueeze(1).to_broadcast([128, nh, 2, 32]), op=ALU.mult), [xq(), rsin_s()], [B_()])
        K.op("dve", lambda e: e.tensor_tensor(out=ov[:, :, :, 0, :], in0=av[:, :, :, 0, :], in1=bv[:, :, :, 0, :], op=ALU.subtract), [A_(), B_()], [out_bf()])
        K.op("dve", lambda e: e.tensor_tensor(out=ov[:, :, :, 1, :], in0=av[:, :, :, 1, :], in1=bv[:, :, :, 1, :], op=ALU.add), [A_(), B_()], [out_bf()])

    def qk_norm(P, src, nh, gain, outf):
        ps_ap = src.ap
        sq = P["ropeA"].next()
        K.op("act", lambda e: e.activation(out=sq.t[:, 0:nh * 128], in_=ps_ap, func=AF.Square), [src], [sq()])
        s8 = P["s8"].next()
        K.op("dve", lambda e: e.tensor_reduce(out=s8.t[:, 0:nh], in_=sq.t[:, 0:nh * 128].rearrange("p (h c) -> p h c", h=nh), axis=AX.X, op=ALU.add), [sq()], [s8()])
        K.op("act", lambda e: e.activation(out=s8.t[:, 0:nh], in_=s8.t[:, 0:nh], func=AF.Sqrt, scale=1.0 / 128, bias=eps_s.t[:, 0:1]), [s8(), eps_s()], [s8()])
        K.op("dve", lambda e: e.reciprocal(out=s8.t[:, 8:8 + nh], in_=s8.t[:, 0:nh]), [s8()], [s8()])
        ov = outf.t[:, 0:nh * 128].rearrange("p (h c) -> p h c", h=nh)
        K.op("dve", lambda e: e.tensor_tensor(out=ov, in0=ps_ap.rearrange("p (h c) -> p h c", h=nh), in1=s8.t[:, 8:8 + nh].unsqueeze(2).to_broadcast([128, nh, 128]), op=ALU.mult), [s8(), src], [outf()])
        K.op("dve", lambda e: e.tensor_tensor(out=ov, in0=ov, in1=gain.t[:].unsqueeze(1).to_broadcast([128, nh, 128]), op=ALU.mult), [outf(), gain()], [outf()])

    K.push_scope()
    PD = {"ropeA": K.pool("ropeA", [128, 1024], F32, 2), "ropeB": K.pool("ropeB", [128, 1024], F32, 2), "s8": K.pool("s8", [128, 16], F32, 3)}
    wkv = K.sb("wkv", [128, 8, 512], BF16)
    K.dma("sp", wkv(), WB["w_c_kv"]())
    knp = K.pool("knp", [128, 256], F32, 2)
    kbp = K.pool("kbp", [128, 256], BF16, 2)
    vfp = K.pool("vfp", [128, 256], F32, 2)
    vbp = K.pool("vbp", [128, 256], BF16, 2)
    ktp2 = K.pool("ktp2", [128, 2, 128], BF16, 2)
    SkT_view = S_kv.t[0:2048, :].rearrange("(p a) c -> p (a c)", p=128).rearrange("p (k t) -> p k t", k=2)
    for g in range(5):
        tok0 = g * 512
        hT = norm_mod_T([(S_x1, tok0 + i * 128) for i in range(4)], 1, 0, 0 if g < 4 else 1)
        for i in range(4):
            t = g * 4 + i
            ps = PF.next()
            for kt in range(8):
                K.op("pe", lambda e, kt=kt, ps=ps, i=i: e.matmul(ps.t[:], lhsT=hT.t[:, kt, i * 128:(i + 1) * 128], rhs=wkv.t[:, kt, :], start=(kt == 0), stop=(kt == 7)), [hT(key=kt), wkv()], [ps()])
            kn = knp.next()
            qk_norm(PD, ps(np.s_[:, 0:256]), 2, kg_s, kn)
            kb = kbp.next()
            vb = vbp.next()
            if g < 4:
                rope(PD, kn, 2, t, kb)
                K.op("act", lambda e, ps=ps, vb=vb: e.activation(out=vb.t[:], in_=ps.t[:, 256:512], func=AF.Copy), [ps()], [vb()])
                K.dma("pool", S_kv(np.s_[2048 + t * 128:2048 + (t + 1) * 128, :], key=("v", t)), vb())
            else:
                c0 = (t - 16) * 128
                K.dma("pool", ok_(np.s_[c0:c0 + 128, :]), kn())
                vf = vfp.next()
                K.op("act", lambda e, ps=ps, vf=vf: e.activation(out=vf.t[:], in_=ps.t[:, 256:512], func=AF.Copy), [ps()], [vf()])
                K.dma("pool", ov_(np.s_[c0:c0 + 128, :]), vf())
                K.op("dve", lambda e, kn=kn, kb=kb: e.tensor_copy(out=kb.t[:], in_=kn.t[:]), [kn()], [kb()])
                K.op("dve", lambda e, vf=vf, vb=vb: e.tensor_copy(out=vb.t[:], in_=vf.t[:]), [vf()], [vb()])
                K.dma("pool", S_cv.v(S_cv.t[c0:c0 + 128].rearrange("t k d -> t (k d)"), key=t), vb())
            pb = PB.next()
            for kv in range(2):
                K.op("pe", lambda e, kv=kv, kb=kb, pb=pb: e.transpose(out=pb.t[:, kv * 128:(kv + 1) * 128], in_=kb.t[:, kv * 128:(kv + 1) * 128], identity=identb.t[:]), [kb(), identb()], [pb()])
            kT2 = ktp2.next()
            K.op("act", lambda e, pb=pb, kT2=kT2: e.activation(out=kT2.t[:].rearrange("p k t -> p (k t)"), in_=pb.t[:, 0:256], func=AF.Copy), [pb()], [kT2()])
            if g < 4:
                K.dma("pool", S_kv.v(SkT_view[:, :, t * 128:(t + 1) * 128], key=("k", t)), kT2())
            else:
                c0 = (t - 16) * 128
                K.dma("pool", S_ckT(np.s_[:, :, c0:c0 + 128], key=t), kT2())
    K.pop_scope()
    K.cc(lambda e: e.collective_compute("AllGather", ALU.bypass, replica_groups=[[0, 1], [2, 3], [4, 5], [6, 7]], ins=[S_kv.t[:, :]], outs=[S_kvg.t[:, :]]),
         [S_kv()], [S_kvg()])

    if stage <= 9:
        return K, es, locals()
    K.push_scope()
    P = post_scope_alloc(need_aT=False)
    P.update({"ropeA": K.pool("ropeA2", [128, 1024], F32, 1), "ropeB": K.pool("ropeB2", [128, 1024], F32, 1), "s8": K.pool("s8b", [128, 16], F32, 3)})
    load_wo(P, "w_oc")
    wqp = K.pool("wq", [128, 8, 512], BF16, 1)
    KT_all = K.sb("KT_all", [128, 2, NKEY], BF16)
    V_all = K.sb("V_all", [128, NKC, 2, 129], BF16)
    K.op("dve", lambda e: e.memset(V_all.t[:, :, :, 128:129], 1.0), [], [V_all(key="one")])
    for r in range(2):
        kview = S_kvg.t[r * 4096:r * 4096 + 2048, :].rearrange("(p a) c -> p (a c)", p=128).rearrange("p (k t) -> p k t", k=2)
        K.dma("sp", KT_all(np.s_[:, :, r * 2048:(r + 1) * 2048], key=("k", r)), S_kvg.v(kview))
        vview = S_kvg.t[r * 4096 + 2048:(r + 1) * 4096, :].rearrange("(c p) (k d) -> p c k d", p=128, k=2)
        for kv in range(2):
            K.dma("sp", V_all(np.s_[:, r * 16:(r + 1) * 16, kv, 0:128], key=("v", r, kv)), S_kvg.v(vview[:, :, kv, :]))
    K.dma("pool", KT_all(np.s_[:, :, 4096:NKEY], key=("k", 2)), ckT())
    for kv in range(2):
        K.dma("pool", V_all(np.s_[:, 32:34, kv, 0:128], key=("v", 2, kv)), cv(np.s_[:, :, kv, :]))
    KTc = K.pool("KTc", [128, 2, 256], BF16, 1)
    Vc = K.pool("Vc", [128, 2, 2, 129], BF16, 1)
    qfp = K.pool("qfp", [128, 1024], F32, 1)
    qbp = K.pool("qbp", [128, 1024], BF16, 1)
    QTp = K.pool("QTp", [128, 8, 128], BF16, 1)
    Pp = K.pool("Pp", [128, 512], BF16, 3)
    Op_ = K.pool("Ob", [128, 8, 128], BF16, 1)
    OTp = K.pool("OTp", [128, 8, 128], BF16, 1)
    r8p = K.pool("r8p", [128, 8], F32, 2)
    SCALE = 128 ** -0.5
    for g in range(5):
        tok0 = g * 512
        j = 0 if g < 4 else 1
        hT = norm_mod_T([(S_x1, tok0 + i * 128) for i in range(4)], 1, 0, j)
        x1s = []
        for i in range(4):
            t = g * 4 + i
            qf = qfp.next()
            for half in range(2):
                ps = PF.next()
                wqb = wqp.next()
                K.dma("sp", wqb(), WB["w_c_q"].v(WB["w_c_q"].t[half], key=("w", half)))
                for kt in range(8):
                    K.op("pe", lambda e, kt=kt, ps=ps, i=i, half=half, wqb=wqb: e.matmul(ps.t[:], lhsT=hT.t[:, kt, i * 128:(i + 1) * 128], rhs=wqb.t[:, kt, :], start=(kt == 0), stop=(kt == 7)), [hT(key=kt), wqb()], [ps()])
                K.op("act", lambda e, ps=ps, qf=qf, half=half: e.activation(out=qf.t[:, half * 512:(half + 1) * 512], in_=ps.t[:], func=AF.Copy), [ps()], [qf(key=half)])
            qn = qf
            qk_norm(P, qf(), 8, qg_s, qn)
            qb = qbp.next()
            if g < 4:
                rope(P, qn, 8, t, qb)
                KT_use, V_use, nkc = KT_all, V_all, NKC
                kt_ap = lambda kv, kc: KT_all.t[:, kv, kc * 128:(kc + 1) * 128]
                v_ap = lambda kv, kc: V_all.t[:, kc, kv, :]
            else:
                K.op("dve", lambda e, qn=qn, qb=qb: e.tensor_copy(out=qb.t[:], in_=qn.t[:]), [qn()], [qb()])
                s = (t - 16) // 2
                if (t - 16) % 2 == 0:
                    KTcb = KTc.next()
                    Vcb = Vc.next()
                    K.dma("sp", KTcb(), S_ckT(np.s_[:, :, s * 256:(s + 1) * 256]))
                    K.op("dve", lambda e, Vcb=Vcb: e.memset(Vcb.t[:, :, :, 128:129], 1.0), [], [Vcb()])
                    for kv_ in range(2):
                        K.dma("sp", Vcb(np.s_[:, :, kv_, 0:128]), S_cv.v(S_cv.t[s * 256:(s + 1) * 256].rearrange("(c p) k d -> p c k d", p=128)[:, :, kv_, :]))
                KT_use, V_use, nkc = KTcb, Vcb, 2
                kt_ap = lambda kv, kc, KTcb=KTcb: KTcb.t[:, kv, kc * 128:(kc + 1) * 128]
                v_ap = lambda kv, kc, Vcb=Vcb: Vcb.t[:, kc, kv, :]
            pb = PB.next()
            for h in range(8):
                K.op("pe", lambda e, h=h, qb=qb, pb=pb: e.transpose(out=pb.t[:, h * 128:(h + 1) * 128], in_=qb.t[:, h * 128:(h + 1) * 128], identity=identb.t[:]), [qb(), identb()], [pb()])
            QT = QTp.next()
            K.op("act", lambda e, pb=pb, QT=QT: e.activation(out=QT.t[:].rearrange("p h c -> p (h c)"), in_=pb.t[:], func=AF.Copy), [pb()], [QT()])
            Ob = Op_.next()
            for kv in range(2):
                acc = [PH.next(), PH.next()]
                accv = [b_.t[:, 0:258].rearrange("p (h c) -> p h c", h=2) for b_ in acc]
                psq = []

                def issue_S(kc, kv=kv, QT=QT, kt_ap=kt_ap, KT_use=KT_use):
                    psS = PF.next()
                    K.op("pe", lambda e: e.matmul(psS.t[:], lhsT=kt_ap(kv, kc), rhs=QT.t[:, kv * 4:(kv + 1) * 4, :].rearrange("p h c -> p (h c)"), start=True, stop=True), [KT_use(), QT()], [psS()])
                    psq.append(psS)
                LOOK = 2
                for kc0 in range(min(LOOK, nkc)):
                    issue_S(kc0)
                for kc in range(nkc):
                    if kc + LOOK < nkc:
                        issue_S(kc + LOOK)
                    psS = psq[kc]
                    Pb = Pp.next()
                    K.op("act", lambda e, psS=psS, Pb=Pb: e.activation(out=Pb.t[:], in_=psS.t[:], func=AF.Exp, scale=SCALE), [psS()], [Pb()])
                    for hh in range(4):
                        K.op("pe", lambda e, hh=hh, kv=kv, kc=kc, Pb=Pb, accv=accv, v_ap=v_ap, nkc=nkc: e.matmul(accv[hh // 2][:, hh % 2, :], lhsT=Pb.t[:, hh * 128:(hh + 1) * 128], rhs=v_ap(kv, kc), start=(kc == 0 and hh % 2 == 0), stop=(kc == nkc - 1 and hh % 2 == 1)), [Pb(), V_use()], [acc[hh // 2]()])
                r8 = r8p.next()
                for b2 in range(2):
                    K.op("dve", lambda e, b2=b2, r8=r8, accv=accv: e.reciprocal(out=r8.t[:, 2 * b2:2 * b2 + 2], in_=accv[b2][:, :, 128]), [acc[b2]()], [r8()])
                    K.op("dve", lambda e, b2=b2, r8=r8, accv=accv, kv=kv, Ob=Ob: e.tensor_tensor(out=Ob.t[:, kv * 4 + 2 * b2:kv * 4 + 2 * b2 + 2, :], in0=accv[b2][:, :, 0:128], in1=r8.t[:, 2 * b2:2 * b2 + 2].unsqueeze(2).to_broadcast([128, 2, 128]), op=ALU.mult), [acc[b2](), r8()], [Ob()])
            pb2 = PB.next()
            for h in range(8):
                K.op("pe", lambda e, h=h, Ob=Ob, pb2=pb2: e.transpose(out=pb2.t[:, h * 128:(h + 1) * 128], in_=Ob.t[:, h, :], identity=identb.t[:]), [Ob(), identb()], [pb2()])
            OT = OTp.next()
            K.op("act", lambda e, pb2=pb2, OT=OT: e.activation(out=OT.t[:].rearrange("p h c -> p (h c)"), in_=pb2.t[:], func=AF.Copy), [pb2()], [OT()])
            xt_ = load_x(S_x1, tok0 + i * 128)
            x1s.append(outproj_residual(P, 1, j, OT, 0, xt_()))
        ffn_group(P, 1, j, x1s, tok0, final=True)
    K.pop_scope()
    return K, es, locals()

def _blk_f(W, c0, nblk):
    Wc = W[:, c0:c0 + nblk * 128].reshape(8, 128, nblk, 128)
    return np.ascontiguousarray(Wc.transpose(2, 1, 0, 3))


def _blk_t(W, c0, width):
    return np.ascontiguousarray(W[:, c0:c0 + width].reshape(8, 128, width).transpose(1, 0, 2))


def _fp(v, n):
    return np.ascontiguousarray(v.reshape(n, 128).T)


_NC_CACHE = {}


def prepare(inputs):
    I = {k_: np.asarray(v) for k_, v in inputs.items()}
    f32 = np.float32
    xp, xsam, c = I["x_prompt"], I["x_sample"], I["c"]
    w_mod, b_mod = I["w_mod"], I["b_mod"]
    fidx = (0, 1, 3, 4)
    wmodA = np.stack([np.concatenate([_blk_f(w_mod[l], fi * 1024, 8) for fi in fidx], 0) for l in range(2)], 0)
    bmodA = np.stack([np.concatenate([_fp(b_mod[l, fi * 1024:(fi + 1) * 1024], 8) for fi in fidx], 1) for l in range(2)], 1)
    gcols = (2048, 2560, 5120, 5632)
    wmodG = np.stack([np.stack([_blk_t(w_mod[l], cc_, 512) for cc_ in gcols], 0) for l in range(2)], 0)
    bmodG = np.stack([np.broadcast_to(np.concatenate([b_mod[l, 2048:3072], b_mod[l, 5120:6144]])[None, :], (128, 2048)) for l in range(2)], 0)
    n1g = np.stack([_fp(I["norm1_g"][l], 8) for l in range(2)], 1)
    n2g = np.stack([_fp(I["norm2_g"][l], 8) for l in range(2)], 1)
    Wab = I["w_in_ab"][0]
    bgate = I["b_gate_ab"][0]
    w_ab_t = np.stack([_blk_t(Wab, 1536 + 512 * i_, 512) for i_ in range(3)], 0)
    conv_w = I["conv_w"][0]
    cb = _fp(I["conv_b"][0], 4)
    lng = _fp(I["conv_ln_g"][0], 4)
    lnb = _fp(I["conv_ln_b"][0], 4)
    mng = np.broadcast_to(I["mlstm_norm_g"][0][None, :], (128, 512))
    Woab = I["w_out_ab"][0]
    w_oab = np.stack([_blk_t(Woab, h_ * 512, 512) for h_ in range(2)], 0)
    Wc = I["w_in_c"][0]
    w_c_q = np.stack([_blk_t(Wc, h_ * 512, 512) for h_ in range(2)], 0)
    w_c_kv = _blk_t(Wc, 1024, 512)
    qg = np.broadcast_to(I["q_norm_g"][0][None, :], (128, 128))
    kg = np.broadcast_to(I["k_norm_g"][0][None, :], (128, 128))
    Woc = I["w_out_c"][0]
    w_oc = np.stack([_blk_t(Woc, h_ * 512, 512) for h_ in range(2)], 0)
    wfi = I["w_ffn_in"]
    w_fi = np.stack([np.concatenate([_blk_f(wfi[l], 0, NFB), _blk_f(wfi[l], DFF, NFB)], axis=3) for l in range(2)], 0)
    wfo = I["w_ffn_out"]
    w_fo = np.stack([np.stack([np.ascontiguousarray(wfo[l][:, h_ * 512:(h_ + 1) * 512].reshape(NFB, 128, 512).transpose(1, 0, 2)) for h_ in range(2)], 0) for l in range(2)], 0)
    fng = np.broadcast_to(I["final_norm_g"][None, :], (128, D))
    ident = np.eye(128, dtype=f32)
    s_ = np.arange(128)
    same = (s_[:, None] // 64) == (s_[None, :] // 64)
    maskA = (same & (s_[:, None] <= s_[None, :])).astype(f32)
    maskB = (same & (s_[:, None] >= s_[None, :])).astype(f32)
    inv = (1.0 / (np.float32(10000.0) ** (np.arange(32, dtype=f32) / np.float32(32)))).astype(f32)

    shared = dict(wmodA=wmodA, bmodA=bmodA, wmodG=wmodG, bmodG=bmodG, n1g=n1g, n2g=n2g, w_ab_t=w_ab_t, cb=cb, lng=lng, lnb=lnb,
                  mng=mng, w_oab=w_oab, w_c_q=w_c_q, w_c_kv=w_c_kv, qg=qg, kg=kg, w_oc=w_oc, w_fi=w_fi, w_fo=w_fo, fng=fng,
                  ident=ident, maskA=maskA, maskB=maskB)
    shared = {k_: np.ascontiguousarray(v, dtype=f32) for k_, v in shared.items()}
    base_f = _blk_f(Wab, 0, 16)

    def gate_blocks(rev):
        gi = [(0, 4), (8, 12)] if not rev else [(8, 12), (0, 4)]
        blks = np.zeros((2, 128, 8, 128), f32)
        bgv = np.zeros((128, 2), f32)
        for bi, (i0, _) in enumerate(gi):
            Wi = Wab[:, 3072 + i0:3072 + i0 + 4].reshape(8, 128, 4).transpose(1, 0, 2)
            Wf = Wab[:, 3072 + i0 + 4:3072 + i0 + 8].reshape(8, 128, 4).transpose(1, 0, 2)
            blks[bi, :, :, 0:4] = Wi
            blks[bi, :, :, 32:36] = Wf
            bgv[0:4, bi] = bgate[i0:i0 + 4]
            bgv[32:36, bi] = bgate[i0 + 4:i0 + 8]
        return blks, bgv

    in_maps = []
    for i in range(8):
        b, half = i // 2, i % 2
        rev = half == 1
        if not rev:
            xl = xsam[b, 0:2048]
            halo = xsam[b, 2048:2063]
            pos = np.arange(2048)
        else:
            xl = xsam[b, 2048:4096][::-1]
            halo = xsam[b, 2047:2032:-1]
            pos = 4095 - np.arange(2048)
        xc = [xp[2 * i + s][::-1] if rev else xp[2 * i + s] for s in range(2)]
        hal = np.zeros((128, D), f32)
        hal[0:15] = halo
        xs = np.concatenate([xl, xc[0], xc[1], hal], 0)
        conds = np.stack([c[b], I["c_ctx"]], 0)
        cT = np.ascontiguousarray(conds.reshape(2, 8, 128).transpose(2, 1, 0))
        d_ = 1 if rev else 0
        C0 = I["state_mlstm_C"][b, 0, d_]
        n0 = I["state_mlstm_n"][b, 0, d_]
        st_C = np.concatenate([C0.transpose(1, 0, 2), n0.T[:, :, None]], 2)
        st_m = I["state_mlstm_m"][b, 0, d_].reshape(4, 1)
        sel = np.zeros((128, 2), f32)
        sel[:, 1 - half] = 1.0
        ckT = I["cache_k"][b, 0].transpose(2, 1, 0)
        cv = I["cache_v"][b, 0].reshape(2, 128, 2, 128).transpose(1, 0, 2, 3)
        row = (pos // 64).astype(f32)
        col = (pos % 64).astype(f32)
        ang = np.concatenate([row[:, None] * inv[None, :], col[:, None] * inv[None, :]], 1).astype(f32)
        rcos = np.cos(ang).astype(f32).reshape(NLT, 128, 64).transpose(1, 0, 2)
        rsin = np.sin(ang).astype(f32).reshape(NLT, 128, 64).transpose(1, 0, 2)
        gb, bgv = gate_blocks(rev)
        w_ab_f = np.concatenate([base_f, gb], 0)
        cwj = conv_w[::-1] if rev else conv_w
        cw = cwj.reshape(31, 4, 128).transpose(2, 1, 0)
        m = dict(shared)
        m.update(xs=xs, cT=cT, st_C=st_C, st_m=st_m, sel=sel, ckT=ckT, cv=cv, rcos=rcos, rsin=rsin, w_ab_f=w_ab_f, bg=bgv, cw=cw)
        in_maps.append({k_: np.ascontiguousarray(v, dtype=f32) for k_, v in m.items()})

    return in_maps


def get_nc(stage=99, debug=False):
    key = (stage, debug)
    if key not in _NC_CACHE:
        nc = bass.Bass("TRN2", target_bir_lowering=False)
        K, es, _ = build(nc, stage=stage, debug=debug)
        K.barrier_final()
        K.emit()
        es.close()
        _NC_CACHE[key] = nc
    return _NC_CACHE[key]


def kernel(**inputs):
    in_maps = prepare(inputs)
    f32 = np.float32
    if False:
        nc = bass.Bass("TRN2", target_bir_lowering=False)
        K, es, _ = build(nc)
        K.barrier_final()
        K.emit()
        es.close()
        _NC_CACHE["nc"] = nc
    nc = get_nc()
    res = run_bass_kernel_spmd(nc, in_maps, core_ids=list(range(8)))
    R_ = res.results

    y_prompt = np.zeros((16, 256, D), f32)
    y_sample = np.zeros((4, 4096, D), f32)
    nC = np.zeros((16, 1, 2, 4, 128, 128), f32)
    nn = np.zeros((16, 1, 2, 4, 128), f32)
    nm = np.zeros((16, 1, 2, 4), f32)
    nk = np.zeros((16, 1, 256, 2, 128), f32)
    nv = np.zeros((16, 1, 256, 2, 128), f32)
    for i in range(8):
        b, half = i // 2, i % 2
        rev = half == 1
        r = R_[i]
        yy = r["y"]
        if not rev:
            y_sample[b, 0:2048] = yy[0:2048]
        else:
            y_sample[b, 2048:4096] = yy[0:2048][::-1]
        for s in range(2):
            blk = yy[2048 + s * 256:2048 + (s + 1) * 256]
            kk = r["ok"][s * 256:(s + 1) * 256].reshape(256, 2, 128)
            vv = r["ov"][s * 256:(s + 1) * 256].reshape(256, 2, 128)
            if rev:
                blk, kk, vv = blk[::-1], kk[::-1], vv[::-1]
            y_prompt[2 * i + s] = blk
            nk[2 * i + s, 0] = kk
            nv[2 * i + s, 0] = vv
            for pas in range(2):
                d_ = pas if not rev else 1 - pas
                nC[2 * i + s, 0, d_] = r["oC"][s, pas]
                nn[2 * i + s, 0, d_] = r["on"][s, pas]
                nm[2 * i + s, 0, d_] = r["om"][s, pas]
    return (y_prompt, y_sample, nC, nn, nm, nk, nv)
```
